# Optimizing a Trainium2 kernel written in Bass

```python
import math
import jax, jax.numpy as jnp
from jax import lax
import numpy as np

D_MODEL = 2048
BATCH = 16
SEQ = 2048
DEPTH = 2

MEM_LEN = 256
D_FF = 5632
BLK = 128
MLA_HEADS = 8
Q_LORA = 512
KV_LORA = 512
NOPE_DIM = 128
ROPE_DIM = 64
MLA_V_DIM = 128
ROPE_THETA = 10000.0
SGU_CHUNK = 128
SGU_GROUPS = 4
SGU_GROUP_DIM = 128
SGU_WIDTH = SGU_GROUPS * SGU_GROUP_DIM
DIL_PATTERNS = ((128, 1), (512, 4), (2048, 16))
DIL_HEADS_PER_GROUP = 4
DIL_HEAD_DIM = 128
DIL_HEADS = DIL_HEADS_PER_GROUP * len(DIL_PATTERNS)
REL_BUCKETS = 32
REL_MAX_DIST = 2048
XA_HEADS = 4
XA_HEAD_DIM = 128
N_BRANCH = 3
MLA_OUT = MLA_HEADS * MLA_V_DIM
SGU_OUT = SGU_WIDTH
DIL_OUT = DIL_HEADS_PER_GROUP * DIL_HEAD_DIM
MIX_WIDTH = MLA_OUT + SGU_OUT + DIL_OUT
IN_SPLITS = (Q_LORA, KV_LORA, ROPE_DIM, 2 * SGU_WIDTH, 3 * DIL_HEADS * DIL_HEAD_DIM, N_BRANCH * D_MODEL)
N_IN = sum(IN_SPLITS)
ALPHA = (2 * DEPTH) ** 0.25
BETA = (8 * DEPTH) ** -0.25
LN_EPS = 1e-5
RMS_EPS = 1e-6
NEG_INF = -1e30

kernel_name = 'hybrid_mla_sgu_dilated_deepnorm'


def layer_norm(x, g, b):
    xf = x.astype(jnp.float32)
    mu = jnp.mean(xf, -1, keepdims=True)
    var = jnp.mean(jnp.square(xf - mu), -1, keepdims=True)
    return ((xf - mu) * lax.rsqrt(var + LN_EPS)).astype(x.dtype) * g + b


def rms_norm(x, g):
    xf = x.astype(jnp.float32)
    return (xf * lax.rsqrt(jnp.mean(jnp.square(xf), -1, keepdims=True) + RMS_EPS)).astype(x.dtype) * g


def swiglu(x, wg, wu, wd):
    return (jax.nn.silu(x @ wg) * (x @ wu)) @ wd


def rope_tables(seq, dtype):
    inv = ROPE_THETA ** (-jnp.arange(0, ROPE_DIM, 2, dtype=jnp.float32) / ROPE_DIM)
    ang = jnp.arange(seq, dtype=jnp.float32)[:, None] * inv[None, :]
    return jnp.cos(ang).astype(dtype), jnp.sin(ang).astype(dtype)


def apply_rope(t, cos, sin):
    half = t.shape[-1] // 2
    t1, t2 = t[..., :half], t[..., half:]
    c, s = cos[None, :, None, :], sin[None, :, None, :]
    return jnp.concatenate([t1 * c - t2 * s, t1 * s + t2 * c], axis=-1)


def t5_bucket(dist):
    exact = REL_BUCKETS // 2
    df = jnp.maximum(dist, 1).astype(jnp.float32)
    large = exact + (jnp.log(df / exact) / math.log(REL_MAX_DIST / exact) * (REL_BUCKETS - exact)).astype(jnp.int32)
    large = jnp.minimum(large, REL_BUCKETS - 1)
    return jnp.where(dist < exact, dist, large)


def causal_block_attention(q, k, v, scale):
    B, S, H, Dk = q.shape
    nb = S // BLK
    qb = q.reshape(B, nb, BLK, H, Dk).transpose(1, 0, 2, 3, 4)
    kpos = jnp.arange(S)

    def one_block(args):
        qblk, i = args
        s = jnp.einsum('bqhd,bkhd->bhqk', qblk, k).astype(jnp.float32) * scale
        qpos = i * BLK + jnp.arange(BLK)
        s = jnp.where(kpos[None, :] <= qpos[:, None], s, NEG_INF)
        p = jax.nn.softmax(s, axis=-1).astype(v.dtype)
        return jnp.einsum('bhqk,bkhd->bqhd', p, v)

    o = lax.map(one_block, (qb, jnp.arange(nb)))
    return o.transpose(1, 0, 2, 3, 4).reshape(B, S, H, v.shape[-1])


def dilated_group_attention(q, k, v, rel_tab, dil, band):
    B, S, Hg, Dh = q.shape
    L = S // dil
    nb = -(-L // BLK)
    Lp = nb * BLK

    def to_sub(t, front):
        t = t.reshape(B, L, dil, Hg, Dh).transpose(0, 2, 1, 3, 4)
        return jnp.pad(t, ((0, 0), (0, 0), (front, Lp - L), (0, 0), (0, 0)))

    def band_keys(tp):
        prev = tp[:, :, :Lp].reshape(B, dil, nb, BLK, Hg, Dh)
        cur = tp[:, :, BLK:].reshape(B, dil, nb, BLK, Hg, Dh)
        return jnp.concatenate([prev, cur], axis=3)

    qs = to_sub(q, 0).reshape(B, dil, nb, BLK, Hg, Dh)
    kb = band_keys(to_sub(k, BLK))
    vb = band_keys(to_sub(v, BLK))
    qi = jnp.arange(BLK)[:, None]
    ki = jnp.arange(2 * BLK)[None, :]
    j = qi + BLK - ki
    in_band = (j >= 0) & (j <= band)
    bias = rel_tab[t5_bucket(jnp.maximum(j, 0) * dil)].astype(jnp.float32).transpose(2, 0, 1)
    key_real = (jnp.arange(nb)[:, None] > 0) | (ki >= BLK)
    mask = in_band[None] & key_real[:, None, :]
    s = jnp.einsum('brnqhd,brnkhd->brnhqk', qs, kb).astype(jnp.float32) * (DIL_HEAD_DIM ** -0.5) + bias
    s = jnp.where(mask[None, None, :, None], s, NEG_INF)
    lse = jax.nn.logsumexp(s, axis=-1)
    p = jnp.exp(s - lse[..., None]).astype(v.dtype)
    o = jnp.einsum('brnhqk,brnkhd->brnqhd', p, vb)
    o = o.reshape(B, dil, Lp, Hg, Dh)[:, :, :L].transpose(0, 2, 1, 3, 4).reshape(B, S, Hg, Dh)
    lse = lse.transpose(0, 1, 2, 4, 3).reshape(B, dil, Lp, Hg)[:, :, :L].transpose(0, 2, 1, 3).reshape(B, S, Hg)
    return o, lse


def mla_branch(c_q, c_kv, k_pe, q_norm, kv_norm, w_uq, w_ukv, cos, sin):
    B, S, _ = c_q.shape
    q = (rms_norm(c_q, q_norm) @ w_uq).reshape(B, S, MLA_HEADS, NOPE_DIM + ROPE_DIM)
    q = jnp.concatenate([q[..., :NOPE_DIM], apply_rope(q[..., NOPE_DIM:], cos, sin)], axis=-1)
    kv = (rms_norm(c_kv, kv_norm) @ w_ukv).reshape(B, S, MLA_HEADS, NOPE_DIM + MLA_V_DIM)
    k_rope = jnp.broadcast_to(apply_rope(k_pe[:, :, None, :], cos, sin), (B, S, MLA_HEADS, ROPE_DIM))
    k = jnp.concatenate([kv[..., :NOPE_DIM], k_rope], axis=-1)
    o = causal_block_attention(q, k, kv[..., NOPE_DIM:], (NOPE_DIM + ROPE_DIM) ** -0.5)
    return o.reshape(B, S, MLA_OUT)


def spatial_gating(z, ln_g, ln_b, ws, bs):
    B, S, _ = z.shape
    u, v = jnp.split(jax.nn.gelu(z, approximate=False), 2, axis=-1)
    v = layer_norm(v, ln_g, ln_b).reshape(B, S // SGU_CHUNK, SGU_CHUNK, SGU_GROUPS, SGU_GROUP_DIM)
    w = ws * jnp.tril(jnp.ones((SGU_CHUNK, SGU_CHUNK), ws.dtype))
    mixed = jnp.einsum('gts,bcsgd->bctgd', w, v) + bs.T[:, :, None]
    return u * mixed.reshape(B, S, SGU_WIDTH)


def dilated_branch(qkv, rel_bias):
    B, S, _ = qkv.shape
    qkv = qkv.reshape(B, S, 3, DIL_HEADS, DIL_HEAD_DIM)
    outs, lses = [], []
    for g, (window, dil) in enumerate(DIL_PATTERNS):
        hs = slice(g * DIL_HEADS_PER_GROUP, (g + 1) * DIL_HEADS_PER_GROUP)
        o, lse = dilated_group_attention(qkv[:, :, 0, hs], qkv[:, :, 1, hs], qkv[:, :, 2, hs],
                                         rel_bias[:, hs], dil, window // dil)
        outs.append(o)
        lses.append(lse)
    wts = jax.nn.softmax(jnp.stack(lses, 0), axis=0).astype(qkv.dtype)
    o = jnp.sum(wts[..., None] * jnp.stack(outs, 0), axis=0)
    return o.reshape(B, S, DIL_OUT)


def hybrid_mixer(h, w_in, q_norm, kv_norm, w_uq, w_ukv, sgu_g, sgu_b, ws, bs, rel_bias, w_branch, w_out, cos, sin):
    B, S, _ = h.shape
    offs = np.cumsum(IN_SPLITS)[:-1].tolist()
    c_q, c_kv, k_pe, z, qkv, gate_in = jnp.split(h @ w_in, offs, axis=-1)
    o_a = mla_branch(c_q, c_kv, k_pe, q_norm, kv_norm, w_uq, w_ukv, cos, sin)
    o_b = spatial_gating(z, sgu_g, sgu_b, ws, bs)
    o_c = dilated_branch(qkv, rel_bias)
    gates = jax.nn.sigmoid(gate_in).reshape(B, S, N_BRANCH, D_MODEL)
    p_a, p_b, p_c = jnp.split(w_branch, [MLA_OUT, MLA_OUT + SGU_OUT], axis=0)
    merged = gates[:, :, 0] * (o_a @ p_a) + gates[:, :, 1] * (o_b @ p_b) + gates[:, :, 2] * (o_c @ p_c)
    return merged @ w_out


def memory_cross_attention(h, mem, wq, wkv, wo):
    B, S, _ = h.shape
    M = mem.shape[1]
    q = (h @ wq).reshape(B, S, XA_HEADS, XA_HEAD_DIM)
    kv = (mem @ wkv).reshape(B, M, 2, XA_HEADS, XA_HEAD_DIM)
    s = jnp.einsum('bqhd,bkhd->bhqk', q, kv[:, :, 0]).astype(jnp.float32) * (XA_HEAD_DIM ** -0.5)
    p = jax.nn.softmax(s, axis=-1).astype(h.dtype)
    o = jnp.einsum('bhqk,bkhd->bqhd', p, kv[:, :, 1]).reshape(B, S, XA_HEADS * XA_HEAD_DIM)
    return o @ wo


def setup_inputs(seed: int = 0) -> dict:
    key = jax.random.key(seed)
    ks = jax.random.split(key, 24)
    D, F = D_MODEL, D_FF

    def nrm(k, shape, fan_in, gain=1.0):
        return jax.random.normal(k, shape, jnp.float32) * (gain * fan_in ** -0.5)

    def noise(k, shape, scale=0.02):
        return jax.random.normal(k, shape, jnp.float32) * scale

    kb1, kb2, kb3 = jax.random.split(ks[17], 3)
    w_branch = jnp.concatenate([nrm(kb1, (DEPTH, MLA_OUT, D), MLA_OUT),
                                nrm(kb2, (DEPTH, SGU_OUT, D), SGU_OUT),
                                nrm(kb3, (DEPTH, DIL_OUT, D), DIL_OUT)], axis=1)
    return {
        'x': jax.random.normal(ks[0], (BATCH, SEQ, D), jnp.float32),
        'mem': jax.random.normal(ks[1], (BATCH, MEM_LEN, D), jnp.float32),
        'ln_g': 1.0 + noise(ks[2], (DEPTH, 4, D)),
        'ln_b': noise(ks[3], (DEPTH, 4, D)),
        'ffn_wg': nrm(ks[4], (DEPTH, 2, D, F), D),
        'ffn_wu': nrm(ks[5], (DEPTH, 2, D, F), D),
        'ffn_wd': nrm(ks[6], (DEPTH, 2, F, D), F, BETA),
        'w_in': nrm(ks[7], (DEPTH, D, N_IN), D),
        'mla_q_norm': 1.0 + noise(ks[8], (DEPTH, Q_LORA)),
        'mla_kv_norm': 1.0 + noise(ks[9], (DEPTH, KV_LORA)),
        'mla_w_uq': nrm(ks[10], (DEPTH, Q_LORA, MLA_HEADS * (NOPE_DIM + ROPE_DIM)), Q_LORA),
        'mla_w_ukv': nrm(ks[11], (DEPTH, KV_LORA, MLA_HEADS * (NOPE_DIM + MLA_V_DIM)), KV_LORA),
        'sgu_ln_g': 1.0 + noise(ks[12], (DEPTH, SGU_WIDTH)),
        'sgu_ln_b': noise(ks[13], (DEPTH, SGU_WIDTH)),
        'sgu_ws': nrm(ks[14], (DEPTH, SGU_GROUPS, SGU_CHUNK, SGU_CHUNK), SGU_CHUNK),
        'sgu_bs': 1.0 + noise(ks[15], (DEPTH, SGU_GROUPS, SGU_CHUNK)),
        'rel_bias': noise(ks[16], (REL_BUCKETS, DIL_HEADS), 0.5),
        'w_branch': w_branch,
        'w_out': nrm(ks[18], (DEPTH, D, D), D, BETA),
        'xa_wq': nrm(ks[19], (DEPTH, D, XA_HEADS * XA_HEAD_DIM), D),
        'xa_wkv': nrm(ks[20], (DEPTH, D, 2 * XA_HEADS * XA_HEAD_DIM), D),
        'xa_wo': nrm(ks[21], (DEPTH, XA_HEADS * XA_HEAD_DIM, D), XA_HEADS * XA_HEAD_DIM, BETA),
    }


def reference(x, mem, ln_g, ln_b, ffn_wg, ffn_wu, ffn_wd, w_in, mla_q_norm, mla_kv_norm, mla_w_uq, mla_w_ukv,
              sgu_ln_g, sgu_ln_b, sgu_ws, sgu_bs, rel_bias, w_branch, w_out, xa_wq, xa_wkv, xa_wo):
    cos, sin = rope_tables(x.shape[1], x.dtype)
    for l in range(DEPTH):
        x = layer_norm(ALPHA * x + 0.5 * swiglu(x, ffn_wg[l, 0], ffn_wu[l, 0], ffn_wd[l, 0]), ln_g[l, 0], ln_b[l, 0])
        y = hybrid_mixer(x, w_in[l], mla_q_norm[l], mla_kv_norm[l], mla_w_uq[l], mla_w_ukv[l],
                         sgu_ln_g[l], sgu_ln_b[l], sgu_ws[l], sgu_bs[l], rel_bias, w_branch[l], w_out[l], cos, sin)
        x = layer_norm(ALPHA * x + y, ln_g[l, 1], ln_b[l, 1])
        x = layer_norm(ALPHA * x + memory_cross_attention(x, mem, xa_wq[l], xa_wkv[l], xa_wo[l]), ln_g[l, 2], ln_b[l, 2])
        x = layer_norm(ALPHA * x + 0.5 * swiglu(x, ffn_wg[l, 1], ffn_wu[l, 1], ffn_wd[l, 1]), ln_g[l, 3], ln_b[l, 3])
    return x
```

```python
from contextlib import ExitStack
import numpy as np
import concourse.bass as bass
import concourse.mybir as mybir
from concourse.bass_utils import run_bass_kernel_spmd

F32 = mybir.dt.float32
BF16 = mybir.dt.bfloat16
AF = mybir.ActivationFunctionType
ALU = mybir.AluOpType
AX = mybir.AxisListType

D = 2048
DC = 16
S = 2048
DEPTH = 2
MEM = 256
DFF = 5632
FC = 44
ALPHA = (2 * DEPTH) ** 0.25
LN_EPS = 1e-5
RMS_EPS = 1e-6
NIN = 12864


class Prog:
    KD = 8

    def __init__(self, nc, stack):
        self.nc = nc
        self.E = {'pe': nc.tensor, 'act': nc.scalar, 'dve': nc.vector, 'pool': nc.gpsimd, 'sp': nc.sync}
        self.sem = {k: stack.enter_context(nc.semaphore('s_' + k)) for k in ('pe', 'act', 'dve', 'pool')}
        self.cnt = {k: 0 for k in self.sem}
        self.dsem = {q: [stack.enter_context(nc.semaphore('d_%s_%d' % (q, i))) for i in range(self.KD)]
                     for q in ('sp', 'pool', 'poolc')}
        self.qeng = {'sp': 'sp', 'pool': 'pool', 'poolc': 'pool'}
        self.dcnt = {q: 0 for q in self.dsem}
        self.state = {}
        self.waited = {}
        self.nins = 0

    def _wait(self, eng, dep, force=False):
        sem, val, peng = dep
        if peng == eng and not force:
            return
        if peng is not None:
            assert self.cnt[peng] >= val, 'dependency on unsignaled instruction (%s)' % peng
        key = (eng, id(sem))
        if self.waited.get(key, 0) >= val:
            return
        self.E[eng].wait_ge(sem, val)
        self.waited[key] = val

    def _deps(self, reads, writes, acc=False):
        deps = []
        for k in reads:
            st = self.state.get(k)
            if st is not None:
                if st[0] is not None:
                    deps.append(st[0])
                deps.extend(st[2])
                if k.startswith('ps'):
                    deps.extend(st[1])
        for k in writes:
            st = self.state.get(k)
            if st is not None:
                if not acc:
                    if st[0] is not None:
                        deps.append(st[0])
                    deps.extend(st[2])
                deps.extend(st[1])
        return deps

    def _update(self, dep, reads, writes, acc=False):
        for k in writes:
            if acc:
                st = self.state.setdefault(k, [None, [], []])
                st[2] = [d for d in st[2] if d[0] is not dep[0]] + [dep]
            else:
                self.state[k] = [dep, [], []]
        for k in reads:
            if k in writes:
                continue
            st = self.state.setdefault(k, [None, [], []])
            if k.startswith('ps'):
                st[0] = dep
                st[1] = []
            else:
                st[1] = [d for d in st[1] if d[0] is not dep[0]] + [dep]

    def op(self, eng, fn, reads=(), writes=(), signal=True, sync_same=()):
        for d in self._deps(reads, writes):
            self._wait(eng, d)
        for k in sync_same:
            st = self.state.get(k)
            if st is not None and st[0] is not None:
                self._wait(eng, st[0], force=True)
        ins = fn(self.E[eng])
        self.nins += 1
        if signal:
            self.cnt[eng] += 1
            ins.then_inc(self.sem[eng], 1)
            val = self.cnt[eng]
        else:
            val = self.cnt[eng] + 1
        self._update((self.sem[eng], val, eng), reads, writes)
        return ins

    def dma(self, q, out, in_, reads=(), writes=(), acc=False):
        i = self.dcnt[q]
        slot, gen = i % self.KD, i // self.KD
        qe = self.qeng[q]
        if gen > 0:
            self._wait(qe, (self.dsem[q][slot], 16 * gen, None))
        for d in self._deps(reads, writes, acc):
            self._wait(qe, d)
        ins = self.E[qe].dma_start(out=out, in_=in_)
        ins.then_inc(self.dsem[q][slot], 16)
        self.nins += 1
        self.dcnt[q] = i + 1
        self._update((self.dsem[q][slot], 16 * (gen + 1), None), reads, writes, acc)
        return ins

    def barrier(self):
        for e in ('pe', 'act', 'dve', 'pool', 'sp'):
            for o in self.sem:
                if o != e and self.cnt[o] > 0:
                    self._wait(e, (self.sem[o], self.cnt[o], o))
            for q in ('sp', 'pool'):
                n = self.dcnt[q]
                for slot in range(self.KD):
                    k = (n - slot + self.KD - 1) // self.KD
                    if k > 0:
                        self._wait(e, (self.dsem[q][slot], 16 * k, None))

    def finish(self):
        for q in self.dsem:
            n = self.dcnt[q]
            for slot in range(self.KD):
                k = (n - slot + self.KD - 1) // self.KD
                if k > 0:
                    self._wait('sp', (self.dsem[q][slot], 16 * k, None))


_UID = [0]


def sb(st, nc, name, shape, dt):
    _UID[0] += 1
    return st.enter_context(nc.sbuf_tensor('%s_%d' % (name, _UID[0]), list(shape), dt))


class Ctx:
    pass


def ln_stats_chunk(P, G, z_ap, zkey, zsq_t, zsq_key, c, nchunks, T, bankS1, bankS2, acc1=None, acc2=None, mode='dve'):
    P.op('act', lambda e: e.activation(out=zsq_t[:, 0:T], in_=z_ap, func=AF.Square),
         reads=[zkey], writes=[zsq_key])
    if mode == 'pe':
        def emit(c=c, z_ap=z_ap, zkey=zkey, zsq_t=zsq_t, zsq_key=zsq_key):
            P.op('pe', lambda e: e.matmul(G.ps[bankS1][:, 0:T], lhsT=G.ones_f[:, :], rhs=z_ap,
                                          start=(c == 0), stop=(c == nchunks - 1)),
                 reads=[zkey, 'ones_f'], writes=['ps%d' % bankS1], signal=(c == nchunks - 1))
            P.op('pe', lambda e: e.matmul(G.ps[bankS2][:, 0:T], lhsT=G.ones_f[:, :], rhs=zsq_t[:, 0:T],
                                          start=(c == 0), stop=(c == nchunks - 1)),
                 reads=[zsq_key, 'ones_f'], writes=['ps%d' % bankS2], signal=True)
        prev = getattr(G, '_pend_stats', None)
        if prev is not None:
            prev()
        G._pend_stats = emit
        if c == nchunks - 1:
            emit()
            G._pend_stats = None
        return
    if c == 0:
        P.op('dve', lambda e: e.tensor_copy(out=acc1[:, 0:T], in_=z_ap), reads=[zkey], writes=[acc1.name])
        P.op('dve', lambda e: e.tensor_copy(out=acc2[:, 0:T], in_=zsq_t[:, 0:T]), reads=[zsq_key], writes=[acc2.name])
    else:
        P.op('dve', lambda e: e.tensor_tensor(out=acc1[:, 0:T], in0=acc1[:, 0:T], in1=z_ap, op=ALU.add),
             reads=[zkey], writes=[acc1.name])
        P.op('dve', lambda e: e.tensor_tensor(out=acc2[:, 0:T], in0=acc2[:, 0:T], in1=zsq_t[:, 0:T], op=ALU.add),
             reads=[zsq_key], writes=[acc2.name])
    if c == nchunks - 1:
        P.op('pe', lambda e: e.matmul(G.ps[bankS1][:, 0:T], lhsT=G.ones_f[:, :], rhs=acc1[:, 0:T], start=True, stop=True),
             reads=[acc1.name, 'ones_f'], writes=['ps%d' % bankS1], signal=True)
        P.op('pe', lambda e: e.matmul(G.ps[bankS2][:, 0:T], lhsT=G.ones_f[:, :], rhs=acc2[:, 0:T], start=True, stop=True),
             reads=[acc2.name, 'ones_f'], writes=['ps%d' % bankS2], signal=True)


def ln_finish_stats(P, G, T, nfeat, eps, bankS1, bankS2, mean_t, rstd_t, tmp_t, with_mean=True):
    inv = 1.0 / nfeat
    if with_mean:
        P.op('dve', lambda e: e.tensor_scalar(out=mean_t[:, 0:T], in0=G.ps[bankS1][:, 0:T], scalar1=inv, scalar2=None,
                                              op0=ALU.mult),
             reads=['ps%d' % bankS1], writes=[mean_t.name])
        P.op('dve', lambda e: e.tensor_tensor(out=tmp_t[:, 0:T], in0=mean_t[:, 0:T], in1=mean_t[:, 0:T], op=ALU.mult),
             reads=[mean_t.name], writes=[tmp_t.name])
        P.op('dve', lambda e: e.scalar_tensor_tensor(out=rstd_t[:, 0:T], in0=G.ps[bankS2][:, 0:T], scalar=inv,
                                                     in1=tmp_t[:, 0:T], op0=ALU.mult, op1=ALU.subtract),
             reads=['ps%d' % bankS2, tmp_t.name], writes=[rstd_t.name])
        P.op('dve', lambda e: e.tensor_scalar(out=rstd_t[:, 0:T], in0=rstd_t[:, 0:T], scalar1=eps, scalar2=None,
                                              op0=ALU.add),
             reads=[rstd_t.name], writes=[rstd_t.name])
    else:
        P.op('dve', lambda e: e.tensor_scalar(out=rstd_t[:, 0:T], in0=G.ps[bankS2][:, 0:T], scalar1=inv, scalar2=eps,
                                              op0=ALU.mult, op1=ALU.add),
             reads=['ps%d' % bankS2], writes=[rstd_t.name])
    P.op('act', lambda e: e.activation(out=rstd_t[:, 0:T], in_=rstd_t[:, 0:T], func=AF.Sqrt),
         reads=[rstd_t.name], writes=[rstd_t.name])
    P.op('dve', lambda e: e.reciprocal(out=rstd_t[:, 0:T], in_=rstd_t[:, 0:T]),
         reads=[rstd_t.name], writes=[rstd_t.name])


def zk(t, c):
    return '%s_c%d' % (t.name, c)


def ln_apply_jobs(P, G, z_t, T, lnidx, mean_t, rstd_t, xbo_t, dst_f, dst_b, keyf, keyb):
    jobs = []

    def chunk(c):
        zc = z_t[:, c, :]
        k = zk(z_t, c)
        P.op('dve', lambda e: e.tensor_tensor(out=zc, in0=zc, in1=mean_t[:, 0:T], op=ALU.subtract),
             reads=[mean_t.name], writes=[k])
        P.op('dve', lambda e: e.tensor_tensor(out=zc, in0=zc, in1=rstd_t[:, 0:T], op=ALU.mult),
             reads=[rstd_t.name], writes=[k])
        P.op('act', lambda e: e.activation(out=zc, in_=zc, func=AF.Identity,
                                           bias=G.lnb[:, lnidx, c:c + 1], scale=G.lng[:, lnidx, c:c + 1]),
             reads=['lnp'], writes=[k])
        if dst_b is not None:
            P.op('act', lambda e: e.activation(out=xbo_t[:, c, :], in_=zc, func=AF.Copy),
                 reads=[k], writes=[zk(xbo_t, c)])

    def store(c0, c1):
        P.dma('pool', out=dst_f[:, c0:c1, :], in_=z_t[:, c0:c1, :], reads=[zk(z_t, c) for c in range(c0, c1)],
              writes=[keyf], acc=(c0 > 0))
        if dst_b is not None:
            P.dma('pool', out=dst_b[:, c0:c1, :], in_=xbo_t[:, c0:c1, :], reads=[zk(xbo_t, c) for c in range(c0, c1)],
                  writes=[keyb], acc=(c0 > 0))
        if c1 == DC and getattr(G, 'pump', None) is not None and getattr(G, 'pump_ok', True):
            G.pump(2)

    for c in range(DC):
        if c % 4 == 3:
            jobs.append(lambda c=c: (chunk(c), store(c - 3, c + 1)))
        else:
            jobs.append(lambda c=c: chunk(c))
    return jobs


def ffn_phase(P, nc, G, l, i, src_f, dst_f, dst_b, NT, lnidx):
    T = 512
    NB = NT // T
    wgu_w = G.wgu[(l, i)]
    wd_w = G.wd[(l, i)]
    with ExitStack() as st:
        xb = sb(st, nc, 'f_xb', [128, DC, T], BF16)
        z = sb(st, nc, 'f_z', [128, DC, T], F32)
        hT = sb(st, nc, 'f_hT', [128, FC, T], BF16)
        xbo = sb(st, nc, 'f_xbo', [128, DC, T], BF16)
        wgu = [sb(st, nc, 'f_wgu%d' % k, [128, DC, 256], BF16) for k in range(3)]
        wd = [sb(st, nc, 'f_wd%d' % k, [128, FC, 128], BF16) for k in range(2)]
        xc = [sb(st, nc, 'f_xc%d' % k, [128, T], F32) for k in range(4)]
        sg = [sb(st, nc, 'f_sg%d' % k, [128, T], F32) for k in range(2)]
        zsq = [sb(st, nc, 'f_zsq%d' % k, [128, T], F32) for k in range(2)]
        mean_t = sb(st, nc, 'f_mean', [128, T], F32)
        rstd_t = sb(st, nc, 'f_rstd', [128, T], F32)
        tmp_t = sb(st, nc, 'f_tmp', [128, T], F32)
        acc1 = sb(st, nc, 'f_acc1', [128, T], F32)
        acc2 = sb(st, nc, 'f_acc2', [128, T], F32)

        def load_wgu(b, j):
            k = (b * FC + j) % 3
            P.dma('sp', out=wgu[k][:, :, :], in_=wgu_w.tile(j).rearrange('p (k n) -> p k n', n=256),
                  reads=[wgu_w.key(j)], writes=[wgu[k].name])

        def load_wd(b, c):
            k = (b * DC + c) % 2
            P.dma('sp', out=wd[k][:, :, :], in_=wd_w.tile(c).rearrange('p (k n) -> p k n', n=128),
                  reads=[wd_w.key(c)], writes=[wd[k].name])

        pending = []
        for b in range(NB):
            t0 = b * T
            P.dma('sp', out=xb[:, :, :], in_=G.XBv[:, :, t0:t0 + T], reads=['XB_%d' % b], writes=[xb.name])
            if b == 0:
                load_wgu(b, 0)
                load_wgu(b, 1)
            for j in range(FC):
                if j + 2 < FC:
                    load_wgu(b, j + 2)
                elif j + 2 == FC:
                    load_wd(b, 0)
                wt = wgu[(b * FC + j) % 3]
                bg, bu = (0, 1) if j % 2 == 0 else (2, 3)
                for kc in range(DC):
                    P.op('pe', lambda e: e.matmul(G.ps[bg][:, 0:T], lhsT=wt[:, kc, 0:128], rhs=xb[:, kc, :],
                                                  start=(kc == 0), stop=(kc == DC - 1)),
                         reads=[wt.name, xb.name], writes=['ps%d' % bg], signal=(kc == DC - 1))
                for kc in range(DC):
                    P.op('pe', lambda e: e.matmul(G.ps[bu][:, 0:T], lhsT=wt[:, kc, 128:256], rhs=xb[:, kc, :],
                                                  start=(kc == 0), stop=(kc == DC - 1)),
                         reads=[wt.name, xb.name], writes=['ps%d' % bu], signal=(kc == DC - 1))
                sgt = sg[j % 2]
                P.op('act', lambda e: e.activation(out=sgt[:, :], in_=G.ps[bg][:, 0:T], func=AF.Silu),
                     reads=['ps%d' % bg], writes=[sgt.name])
                P.op('dve', lambda e: e.tensor_tensor(out=hT[:, j, :], in0=G.ps[bu][:, 0:T], in1=sgt[:, :], op=ALU.mult),
                     reads=['ps%d' % bu, sgt.name], writes=[hT.name])
                if pending:
                    pending.pop(0)()
                elif j >= 8 and j % 2 == 0 and getattr(G, 'extra_jobs', None):
                    G.extra_jobs.pop(0)()
            while pending:
                pending.pop(0)()
            if getattr(G, 'dbg_h', None) is not None:
                P.dma('pool', out=G.dbg_h, in_=hT[:, :, :], reads=[hT.name], writes=['dbg_h'])
            for c in range(DC):
                if c + 1 < DC:
                    load_wd(b, c + 1)
                if b + 1 < NB and c + 2 == DC:
                    load_wgu(b + 1, 0)
                elif b + 1 < NB and c + 2 == DC + 1:
                    load_wgu(b + 1, 1)
                xct = xc[c % 4]
                P.dma('sp', out=xct[:, :], in_=src_f[:, c, t0:t0 + T], reads=['XT_%d' % b], writes=[xct.name])
                wt = wd[(b * DC + c) % 2]
                bo = 4 + (c % 2)
                for fc in range(FC):
                    P.op('pe', lambda e: e.matmul(G.ps[bo][:, 0:T], lhsT=wt[:, fc, :], rhs=hT[:, fc, :],
                                                  start=(fc == 0), stop=(fc == FC - 1)),
                         reads=[wt.name, hT.name], writes=['ps%d' % bo], signal=(fc == FC - 1))
                P.op('dve', lambda e: e.scalar_tensor_tensor(out=z[:, c, :], in0=xct[:, :], scalar=2.0 * ALPHA,
                                                             in1=G.ps[bo][:, 0:T], op0=ALU.mult, op1=ALU.add),
                     reads=[xct.name, 'ps%d' % bo], writes=[zk(z, c)])
                ln_stats_chunk(P, G, z[:, c, :], zk(z, c), zsq[c % 2], zsq[c % 2].name, c, DC, T, 6, 7, acc1, acc2)
            ln_finish_stats(P, G, T, D, 4.0 * LN_EPS, 6, 7, mean_t, rstd_t, tmp_t)
            G.pump_ok = (b + 1 < NB)
            pending = ln_apply_jobs(P, G, z, T, lnidx, mean_t, rstd_t, xbo,
                                    dst_f[:, :, t0:t0 + T], None if dst_b is None else dst_b[:, :, t0:t0 + T],
                                    'XT_%d' % b, 'XB_%d' % b)
        while pending:
            pending.pop(0)()


def tile_w(W):
    K, N = W.shape
    return np.ascontiguousarray(W.reshape(K // 128, 128, N // 128, 128).transpose(2, 1, 0, 3)).reshape(N // 128, 128, (K // 128) * 128)


def tile_wgu(Wg, Wu):
    K, N = Wg.shape
    g = Wg.reshape(K // 128, 128, N // 128, 128).transpose(2, 1, 0, 3)
    u = Wu.reshape(K // 128, 128, N // 128, 128).transpose(2, 1, 0, 3)
    return np.ascontiguousarray(np.concatenate([g, u], axis=3)).reshape(N // 128, 128, (K // 128) * 256)


def build_globals(P, nc, G, st):
    G.ps = [st.enter_context(nc.psum_tensor('psb%d' % i, [128, 512], F32)) for i in range(8)]
    G.ones_f = sb(st, nc, 'ones_f', [128, 128], F32)
    G.ones_b = sb(st, nc, 'ones_b', [128, 128], BF16)
    P.op('dve', lambda e: e.memset(G.ones_f[:, :], 1.0), writes=['ones_f'])
    P.op('dve', lambda e: e.memset(G.ones_b[:, :], 1.0), writes=['ones_b'])
    G.lng = sb(st, nc, 'lng_sb', [128, 8, DC], F32)
    G.lnb = sb(st, nc, 'lnb_sb', [128, 8, DC], F32)
    P.dma('sp', out=G.lng[:, :, :], in_=G.lng_d, writes=['lnp'])
    P.dma('sp', out=G.lnb[:, :, :], in_=G.lnb_d, writes=['lnp2'])


class WT:
    def __init__(self, nc, name, n, F, group):
        self.name, self.n, self.F, self.group = name, n, F, group
        self.src = nc.dram_tensor(name, [n, 128, F], F32, kind='ExternalInput').ap()
        self.dst = nc.dram_tensor(name + '_bf', [n, 128, F], BF16, kind='Internal').ap()
        self.issued = set()

    def jobs(self):
        g = self.group
        out = []
        for i in range(0, self.n, g):
            out.append((self, i))
        return out

    def issue(self, P, i):
        g = self.group
        e = min(self.n, i + g)
        P.dma('poolc', out=self.dst[i:e], in_=self.src[i:e], writes=['%s_%d' % (self.name, i // g)])
        self.issued.add(i // g)

    @property
    def done(self):
        return len(self.issued) == (self.n + self.group - 1) // self.group

    def cast(self, P):
        for w, i in self.jobs():
            if (i // self.group) not in self.issued:
                self.issue(P, i)

    def key(self, j):
        return '%s_%d' % (self.name, j // self.group)

    def tile(self, j):
        return self.dst[j]


MLA_SCALE = (128 + 64) ** -0.5


def evac_copy(P, eng, out_ap, out_key, bank, T, lo=0, np_=128):
    if eng == 'act':
        P.op('act', lambda e: e.activation(out=out_ap, in_=P.G.ps[bank][0:np_, lo:T], func=AF.Copy),
             reads=['ps%d' % bank], writes=[out_key])
    else:
        P.op('dve', lambda e: e.tensor_copy(out=out_ap, in_=P.G.ps[bank][0:np_, lo:T]),
             reads=['ps%d' % bank], writes=[out_key])


def rope_combine(P, G, bankA, bankB, out_ap, out_key, tcol, T, tmpa, tmpb):
    P.op('dve', lambda e: e.tensor_tensor(out=tmpa[0:64, 0:T], in0=G.ps[bankA][0:64, 0:T], in1=G.cos_t[0:64, tcol:tcol + T],
                                          op=ALU.mult), reads=['ps%d' % bankA, 'rope'], writes=[tmpa.name])
    P.op('dve', lambda e: e.tensor_tensor(out=tmpb[0:64, 0:T], in0=G.ps[bankB][0:64, 0:T], in1=G.sin_t[0:64, tcol:tcol + T],
                                          op=ALU.mult), reads=['ps%d' % bankB, 'rope'], writes=[tmpb.name])
    P.op('dve', lambda e: e.tensor_tensor(out=out_ap, in0=tmpa[0:64, 0:T], in1=tmpb[0:64, 0:T], op=ALU.add),
         reads=[tmpa.name, tmpb.name], writes=[out_key])


def mla_phase(P, nc, G, l, seqs):
    T = 512
    NBLK = S // T
    wm = G.win_mla[l]
    with ExitStack() as st:
        cqn = sb(st, nc, 'm_cqn', [128, 4, S], BF16)
        ckvn = sb(st, nc, 'm_ckvn', [128, 4, S], BF16)
        krot = sb(st, nc, 'm_krot', [64, S], BF16)
        xb = [sb(st, nc, 'm_xb%d' % k, [128, DC, T], BF16) for k in range(2)]
        wt = [sb(st, nc, 'm_wt%d' % k, [128, DC, 128], BF16) for k in range(3)]
        craw = sb(st, nc, 'm_craw', [128, 4, T], F32)
        zsq = [sb(st, nc, 'm_zsq%d' % k, [128, T], F32) for k in range(2)]
        rstd = sb(st, nc, 'm_rstd', [128, T], F32)
        tmpa = sb(st, nc, 'm_tmpa', [64, T], F32)
        tmpb = sb(st, nc, 'm_tmpb', [64, T], F32)
        qn = sb(st, nc, 'm_qn', [128, S], BF16)
        qr = sb(st, nc, 'm_qr', [64, S], BF16)
        kn = sb(st, nc, 'm_kn', [128, S], BF16)
        vh = sb(st, nc, 'm_vh', [128, 16, 128], BF16)
        wq = [sb(st, nc, 'm_wq%d' % k, [128, 4, 128], BF16) for k in range(2)]
        wkv = [sb(st, nc, 'm_wkv%d' % k, [128, 4, 128], BF16) for k in range(2)]
        pt = [sb(st, nc, 'm_pt%d' % k, [128, T], BF16) for k in range(4)]
        rec = sb(st, nc, 'm_rec', [128, T], F32)
        ot = [sb(st, nc, 'm_ot%d' % k, [128, T], BF16) for k in range(2)]
        G.cos_t = sb(st, nc, 'm_cos', [64, S], F32)
        G.sin_t = sb(st, nc, 'm_sin', [64, S], F32)

        for s in seqs:
            tb = s * S
            sched = [(b_, oc_) for b_ in range(NBLK) for oc_ in range(9)]

            def issue_w(i_):
                w_ = wt[i_ % 3]
                P.dma('sp', out=w_[:, :, :], in_=wm.tile(sched[i_][1]).rearrange('p (k n) -> p k n', n=128),
                      reads=[wm.key(sched[i_][1])], writes=[w_.name])

            issue_w(0)
            P.dma('sp', out=xb[0][:, :, :], in_=G.XBv[:, :, tb:tb + T], reads=['XB_%d' % (tb // T)], writes=[xb[0].name])
            issue_w(1)
            if s == seqs[0]:
                P.dma('sp', out=G.cos_t[:, :], in_=G.cos_d, writes=['rope'])
                P.dma('sp', out=G.sin_t[:, :], in_=G.sin_d, writes=['rope'], acc=True)
            for b in range(NBLK):
                xbt = xb[b % 2]
                if b + 1 < NBLK:
                    t1 = tb + (b + 1) * T
                    P.dma('sp', out=xb[(b + 1) % 2][:, :, :], in_=G.XBv[:, :, t1:t1 + T], reads=['XB_%d' % (t1 // T)],
                          writes=[xb[(b + 1) % 2].name])
                for oc in range(9):
                    i_cur = b * 9 + oc
                    w = wt[i_cur % 3]
                    if i_cur + 2 < len(sched):
                        issue_w(i_cur + 2)
                    if oc < 8:
                        bank = oc % 2
                        for kc in range(DC):
                            P.op('pe', lambda e: e.matmul(G.ps[bank][:, 0:T], lhsT=w[:, kc, :], rhs=xbt[:, kc, :],
                                                          start=(kc == 0), stop=(kc == DC - 1)),
                                 reads=[w.name, xbt.name], writes=['ps%d' % bank], signal=(kc == DC - 1))
                        c4 = oc % 4
                        evac_copy(P, 'act', craw[:, c4, :], craw.name, bank, T)
                        z_ap = craw[:, c4, :]
                        zq = zsq[oc % 2]
                        P.op('act', lambda e: e.activation(out=zq[:, :], in_=z_ap, func=AF.Square),
                             reads=[craw.name], writes=[zq.name])
                        P.op('pe', lambda e: e.matmul(G.ps[2][:, 0:T], lhsT=G.ones_f[:, :], rhs=zq[:, :],
                                                      start=(c4 == 0), stop=(c4 == 3)),
                             reads=[zq.name, 'ones_f'], writes=['ps2'], signal=True)
                        if c4 == 3:
                            ln_finish_stats(P, G, T, 512, RMS_EPS, None, 2, None, rstd, None, with_mean=False)
                            dstt = cqn if oc < 4 else ckvn
                            nrm = G.qnorm if oc < 4 else G.kvnorm
                            for cc in range(4):
                                P.op('dve', lambda e: e.scalar_tensor_tensor(
                                    out=dstt[:, cc, b * T:(b + 1) * T], in0=craw[:, cc, :], scalar=nrm[:, l, cc:cc + 1],
                                    in1=rstd[:, :], op0=ALU.mult, op1=ALU.mult),
                                    reads=[craw.name, rstd.name, 'smallp'], writes=[dstt.name])
                    else:
                        for half, bank in ((0, 3), (1, 4)):
                            for kc in range(DC):
                                P.op('pe', lambda e: e.matmul(G.ps[bank][0:64, 0:T], lhsT=w[:, kc, half * 64:half * 64 + 64],
                                                              rhs=xbt[:, kc, :], start=(kc == 0), stop=(kc == DC - 1)),
                                     reads=[w.name, xbt.name], writes=['ps%d' % bank], signal=(kc == DC - 1))
                        rope_combine(P, G, 3, 4, krot[0:64, b * T:(b + 1) * T], krot.name, b * T, T, tmpa, tmpb)

            for h in range(8):
                if 1 <= h <= 6 and getattr(G, 'pump', None) is not None:
                    G.pump(1)
                wqa, wqb = wq[0], wq[1]
                wka, wva = wkv[0], wkv[1]
                P.dma('sp', out=wqa[:, :, :], in_=G.wuq[l].tile(2 * h).rearrange('p (k n) -> p k n', n=128),
                      reads=[G.wuq[l].key(2 * h)], writes=[wqa.name])
                P.dma('sp', out=wqb[:, :, :], in_=G.wuq[l].tile(2 * h + 1).rearrange('p (k n) -> p k n', n=128),
                      reads=[G.wuq[l].key(2 * h + 1)], writes=[wqb.name])
                P.dma('sp', out=wka[:, :, :], in_=G.wukv[l].tile(2 * h).rearrange('p (k n) -> p k n', n=128),
                      reads=[G.wukv[l].key(2 * h)], writes=[wka.name])
                P.dma('sp', out=wva[:, :, :], in_=G.wukv[l].tile(2 * h + 1).rearrange('p (k n) -> p k n', n=128),
                      reads=[G.wukv[l].key(2 * h + 1)], writes=[wva.name])
                for b in range(NBLK):
                    cs = slice(b * T, (b + 1) * T)
                    for kc in range(4):
                        P.op('pe', lambda e: e.matmul(G.ps[7][:, 0:T], lhsT=wqa[:, kc, :], rhs=cqn[:, kc, cs],
                                                      start=(kc == 0), stop=(kc == 3)),
                             reads=[wqa.name, cqn.name], writes=['ps7'], signal=(kc == 3))
                    evac_copy(P, 'act', qn[:, cs], qn.name, 7, T)
                    for half, bank in ((0, 5), (1, 6)):
                        for kc in range(4):
                            P.op('pe', lambda e: e.matmul(G.ps[bank][0:64, 0:T], lhsT=wqb[:, kc, half * 64:half * 64 + 64],
                                                          rhs=cqn[:, kc, cs], start=(kc == 0), stop=(kc == 3)),
                                 reads=[wqb.name, cqn.name], writes=['ps%d' % bank], signal=(kc == 3))
                    rope_combine(P, G, 5, 6, qr[0:64, cs], qr.name, b * T, T, tmpa, tmpb)
                    for kc in range(4):
                        P.op('pe', lambda e: e.matmul(G.ps[2][:, 0:T], lhsT=wka[:, kc, :], rhs=ckvn[:, kc, cs],
                                                      start=(kc == 0), stop=(kc == 3)),
                             reads=[wka.name, ckvn.name], writes=['ps2'], signal=(kc == 3))
                    evac_copy(P, 'dve', kn[:, cs], kn.name, 2, T)
                    for tt in range(4):
                        tok = slice(b * T + tt * 128, b * T + (tt + 1) * 128)
                        for kc in range(4):
                            P.op('pe', lambda e: e.matmul(G.ps[1][:, tt * 128:(tt + 1) * 128], lhsT=ckvn[:, kc, tok],
                                                          rhs=wva[:, kc, :], start=(kc == 0), stop=(kc == 3)),
                                 reads=[wva.name, ckvn.name], writes=['ps1'], signal=(tt == 3 and kc == 3))
                    P.op('act', lambda e: e.activation(out=vh[:, 4 * b:4 * b + 4, :].rearrange('p a b -> p (a b)'),
                                                       in_=G.ps[1][:, 0:T], func=AF.Copy),
                         reads=['ps1'], writes=[vh.name])
                tiles = [(qb, kc) for qb in range(NBLK) for kc in range(4 * qb + 4)]

                def emit_scores(idx):
                    qb, kc = tiles[idx]
                    lo = max(0, kc - 4 * qb) * 128
                    bank = idx % 3
                    qs = slice(qb * T + lo, (qb + 1) * T)
                    ks = slice(kc * 128, (kc + 1) * 128)
                    P.op('pe', lambda e: e.matmul(G.ps[bank][:, lo:T], lhsT=kn[:, ks], rhs=qn[:, qs], start=True, stop=False),
                         reads=[kn.name, qn.name], writes=['ps%d' % bank], signal=False)
                    P.op('pe', lambda e: e.matmul(G.ps[bank][:, lo:T], lhsT=krot[0:64, ks], rhs=qr[0:64, qs],
                                                  start=False, stop=True),
                         reads=[krot.name, qr.name], writes=['ps%d' % bank], signal=True)

                emit_scores(0)
                emit_scores(1)
                for idx, (qb, kc) in enumerate(tiles):
                    if idx + 2 < len(tiles):
                        emit_scores(idx + 2)
                    lo = max(0, kc - 4 * qb) * 128
                    bank = idx % 3
                    p_t = pt[idx % 4]
                    P.op('act', lambda e: e.activation(out=p_t[:, lo:T], in_=G.ps[bank][:, lo:T], func=AF.Exp, scale=MLA_SCALE),
                         reads=['ps%d' % bank], writes=[p_t.name])
                    if kc >= 4 * qb:
                        P.op('dve', lambda e: e.tensor_tensor(out=p_t[:, lo:lo + 128], in0=p_t[:, lo:lo + 128],
                                                              in1=G.tri_b[:, :], op=ALU.mult),
                             reads=['tri_b'], writes=[p_t.name])
                    bo, bl = 3 + (qb % 2), 5 + (qb % 2)
                    last = (kc == 4 * qb + 3)
                    P.op('pe', lambda e: e.matmul(G.ps[bo][:, lo:T], lhsT=vh[:, kc, :], rhs=p_t[:, lo:T],
                                                  start=(kc == 0), stop=last),
                         reads=[vh.name, p_t.name], writes=['ps%d' % bo], signal=last)
                    P.op('pe', lambda e: e.matmul(G.ps[bl][:, lo:T], lhsT=G.ones_b[:, :], rhs=p_t[:, lo:T],
                                                  start=(kc == 0), stop=last),
                         reads=['ones_b', p_t.name], writes=['ps%d' % bl], signal=True)
                    if last:
                        o_t = ot[qb % 2]
                        P.op('dve', lambda e: e.reciprocal(out=rec[:, :], in_=G.ps[bl][:, 0:T]),
                             reads=['ps%d' % bl], writes=[rec.name])
                        P.op('dve', lambda e: e.tensor_tensor(out=o_t[:, :], in0=G.ps[bo][:, 0:T], in1=rec[:, :], op=ALU.mult),
                             reads=['ps%d' % bo, rec.name], writes=[o_t.name])
                        t0 = tb + qb * T
                        P.dma('pool', out=G.OA[h * 128:(h + 1) * 128, t0:t0 + T], in_=o_t[:, :],
                              reads=[o_t.name], writes=['OA_%d' % (t0 // T)], acc=True)


def sgu_phase(P, nc, G, l, seqs):
    T = 512
    NBLK = S // T
    wu_w = G.win_u[l]
    wv_w = G.win_v[l]
    with ExitStack() as st:
        xb = [sb(st, nc, 's_xb%d' % k, [128, DC, T], BF16) for k in range(2)]
        wv = sb(st, nc, 's_wv', [128, DC, 512], BF16)
        wu = [sb(st, nc, 's_wu%d' % k, [128, DC, 128], BF16) for k in range(4)]
        u_t = sb(st, nc, 's_u', [128, 4, T], F32)
        vt = [sb(st, nc, 's_vt%d' % k, [128, T], F32) for k in range(2)]
        vn = [sb(st, nc, 's_vn%d' % k, [128, T], BF16) for k in range(4)]
        stats = sb(st, nc, 's_stats', [128, 6], F32)
        mv = sb(st, nc, 's_mv', [128, 2], F32)
        rs = sb(st, nc, 's_rs', [128, 1], F32)
        tmp = sb(st, nc, 's_tmp', [128, T], F32)
        ob = sb(st, nc, 's_ob', [128, 4, T], BF16)
        sgu_g = sb(st, nc, 's_g', [128, 512], F32)
        sgu_bb = sb(st, nc, 's_b', [128, 512], F32)
        wsf = sb(st, nc, 's_wsf', [128, 4, 128], F32)
        wsT = sb(st, nc, 's_wsT', [128, 4, 128], BF16)
        bs_bc = sb(st, nc, 's_bs', [128, 4, 4, 128], F32)
        P.dma('sp', out=sgu_g[:, :], in_=G.sgu_g_d[l:l + 1, :].to_broadcast([128, 512]), writes=['smallp_s'])
        P.dma('sp', out=sgu_bb[:, :], in_=G.sgu_b_d[l:l + 1, :].to_broadcast([128, 512]), writes=['smallp_s'], acc=True)
        P.dma('sp', out=wsf[:, :, :], in_=G.wsT_d[l], writes=[wsf.name])
        for rep in range(4):
            P.dma('sp', out=bs_bc[:, :, rep, :], in_=G.bs_d[l:l + 1].to_broadcast([128, 4, 128]),
                  writes=['smallp_s'], acc=True)
        for g in range(4):
            P.op('dve', lambda e: e.tensor_tensor(out=wsT[:, g, :], in0=wsf[:, g, :], in1=G.tri_f[:, :], op=ALU.mult),
                 reads=[wsf.name, 'tri_f'], writes=['wsT'])

        for g in range(4):
            P.dma('sp', out=wu[g][:, :, :], in_=wu_w.tile(g).rearrange('p (k n) -> p k n', n=128),
                  reads=[wu_w.key(g)], writes=[wu[g].name])
        P.dma('sp', out=wv[:, :, :], in_=wv_w.tile(0).rearrange('p (k n) -> p k n', n=512),
              reads=[wv_w.key(0)], writes=[wv.name])
        for s in seqs:
            tb = s * S
            P.dma('sp', out=xb[0][:, :, :], in_=G.XBv[:, :, tb:tb + T], reads=['XB_%d' % (tb // T)], writes=[xb[0].name])
            for b in range(NBLK):
                xbt = xb[b % 2]
                if b + 1 < NBLK:
                    t1 = tb + (b + 1) * T
                    P.dma('sp', out=xb[(b + 1) % 2][:, :, :], in_=G.XBv[:, :, t1:t1 + T], reads=['XB_%d' % (t1 // T)],
                          writes=[xb[(b + 1) % 2].name])
                for g in range(4):
                    bank = g % 2
                    for kc in range(DC):
                        P.op('pe', lambda e: e.matmul(G.ps[bank][:, 0:T], lhsT=wu[g][:, kc, :], rhs=xbt[:, kc, :],
                                                      start=(kc == 0), stop=(kc == DC - 1)),
                             reads=[wu[g].name, xbt.name], writes=['ps%d' % bank], signal=(kc == DC - 1))
                    P.op('act', lambda e: e.activation(out=u_t[:, g, :], in_=G.ps[bank][:, 0:T], func=AF.Gelu),
                         reads=['ps%d' % bank], writes=[u_t.name])
                for tt in range(4):
                    bank = 2 + (tt % 2)
                    for kc in range(DC):
                        P.op('pe', lambda e: e.matmul(G.ps[bank][:, 0:T], lhsT=xbt[:, kc, tt * 128:(tt + 1) * 128],
                                                      rhs=wv[:, kc, :], start=(kc == 0), stop=(kc == DC - 1)),
                             reads=[wv.name, xbt.name], writes=['ps%d' % bank], signal=(kc == DC - 1))
                    v_t = vt[tt % 2]
                    P.op('act', lambda e: e.activation(out=v_t[:, :], in_=G.ps[bank][:, 0:T], func=AF.Gelu),
                         reads=['ps%d' % bank], writes=[v_t.name])
                    P.op('dve', lambda e: e.bn_stats(out=stats[:, :], in_=v_t[:, :]), reads=[v_t.name], writes=[stats.name])
                    P.op('dve', lambda e: e.bn_aggr(out=mv[:, :], in_=stats[:, :]), reads=[stats.name], writes=[mv.name],
                         sync_same=[stats.name])
                    P.op('dve', lambda e: e.tensor_scalar(out=rs[:, :], in0=mv[:, 1:2], scalar1=LN_EPS, scalar2=None, op0=ALU.add),
                         reads=[mv.name], writes=[rs.name], sync_same=[mv.name])
                    P.op('act', lambda e: e.activation(out=rs[:, :], in_=rs[:, :], func=AF.Sqrt), reads=[rs.name], writes=[rs.name])
                    P.op('dve', lambda e: e.reciprocal(out=rs[:, :], in_=rs[:, :]), reads=[rs.name], writes=[rs.name])
                    P.op('dve', lambda e: e.tensor_scalar(out=v_t[:, :], in0=v_t[:, :], scalar1=mv[:, 0:1], scalar2=rs[:, 0:1],
                                                          op0=ALU.subtract, op1=ALU.mult),
                         reads=[mv.name, rs.name], writes=[v_t.name], sync_same=[mv.name, rs.name])
                    P.op('dve', lambda e: e.tensor_tensor(out=v_t[:, :], in0=v_t[:, :], in1=sgu_g[:, :], op=ALU.mult),
                         reads=['smallp_s'], writes=[v_t.name])
                    P.op('dve', lambda e: e.tensor_tensor(out=vn[tt][:, :], in0=v_t[:, :], in1=sgu_bb[:, :], op=ALU.add),
                         reads=['smallp_s', v_t.name], writes=[vn[tt].name])
                for g in range(4):
                    bank = 4 + (g % 2)
                    for tt in range(4):
                        P.op('pe', lambda e: e.matmul(G.ps[bank][:, tt * 128:(tt + 1) * 128], lhsT=vn[tt][:, g * 128:(g + 1) * 128],
                                                      rhs=wsT[:, g, :], start=True, stop=True),
                             reads=[vn[tt].name, 'wsT'], writes=['ps%d' % bank], signal=(tt == 3))
                    P.op('dve', lambda e: e.tensor_tensor(out=tmp[:, :], in0=G.ps[bank][:, 0:T],
                                                          in1=bs_bc[:, g, :, :].rearrange('p a b -> p (a b)'), op=ALU.add),
                         reads=['ps%d' % bank, 'smallp_s'], writes=[tmp.name])
                    P.op('dve', lambda e: e.tensor_tensor(out=ob[:, g, :], in0=tmp[:, :], in1=u_t[:, g, :], op=ALU.mult),
                         reads=[tmp.name, u_t.name], writes=[ob.name])
                t0 = tb + b * T
                P.dma('pool', out=G.OBv[:, :, t0:t0 + T], in_=ob[:, :, :], reads=[ob.name], writes=['OB_%d' % (t0 // T)])


DIL = ((1, 16), (4, 4), (16, 1))
DIL_SCALE = 128 ** -0.5


def dil_phase(P, nc, G, l, seqs):
    T = 512
    NBLK = S // T
    wq_w = G.win_qkv[l]
    with ExitStack() as st:
        xb = [sb(st, nc, 'd_xb%d' % k, [128, DC, T], BF16) for k in range(2)]
        wt = [sb(st, nc, 'd_wt%d' % k, [128, DC, 128], BF16) for k in range(3)]
        qkv2 = [sb(st, nc, 'd_qkv%d' % k, [128, 9, S], BF16) for k in range(2)]
        vtm2 = [sb(st, nc, 'd_vtm%d' % k, [128, 3, 16, 128], BF16) for k in range(2)]
        nd = sb(st, nc, 'd_nd', [128, 2, S], F32)
        e_t = [sb(st, nc, 'd_e%d' % k, [128, 256], F32) for k in range(4)]
        pt = [sb(st, nc, 'd_pt%d' % k, [128, 256], BF16) for k in range(4)]
        rec = sb(st, nc, 'd_rec', [128, T], F32)
        oc_t = [sb(st, nc, 'd_oc%d' % k, [128, T], BF16) for k in range(2)]
        psT = G.ps[2][:, :].bitcast(BF16)
        G.E = sb(st, nc, 'd_E', [128, 12, 256], F32)
        items = [(s_, hh_) for s_ in seqs for hh_ in range(4)]
        wcount = [0]
        xcount = [0]

        def stage_a(k):
            s_, hh = items[k]
            tb = s_ * S
            qkv, vtm = qkv2[k % 2], vtm2[k % 2]
            jobs = []
            sched = [(b_, oc_) for b_ in range(NBLK) for oc_ in range(9)]
            base = wcount[0]
            wcount[0] += len(sched)
            xbase = xcount[0]
            xcount[0] += NBLK

            def issue_w(i_):
                w_ = wt[(base + i_) % 3]
                ti = hh * 9 + sched[i_][1]
                P.dma('sp', out=w_[:, :, :], in_=wq_w.tile(ti).rearrange('p (k n) -> p k n', n=128),
                      reads=[wq_w.key(ti)], writes=[w_.name])

            def load_xb(b):
                t1 = tb + b * T
                x_ = xb[(xbase + b) % 2]
                P.dma('sp', out=x_[:, :, :], in_=G.XBv[:, :, t1:t1 + T], reads=['XB_%d' % (t1 // T)], writes=[x_.name])

            def group(i_cur):
                b, oc = sched[i_cur]
                if i_cur == 0:
                    issue_w(0)
                    load_xb(0)
                    issue_w(1)
                    if k == 0:
                        P.dma('sp', out=G.E[:, :, :], in_=G.E_d, reads=['E_d'], writes=['E'])
                if oc == 0 and b + 1 < NBLK:
                    load_xb(b + 1)
                if i_cur + 2 < len(sched):
                    issue_w(i_cur + 2)
                w = wt[(base + i_cur) % 3]
                xbt = xb[(xbase + b) % 2]
                bank = i_cur % 2
                for kc in range(DC):
                    P.op('pe', lambda e: e.matmul(G.ps[bank][:, 0:T], lhsT=w[:, kc, :], rhs=xbt[:, kc, :],
                                                  start=(kc == 0), stop=(kc == DC - 1)),
                         reads=[w.name, xbt.name], writes=['ps%d' % bank], signal=(kc == DC - 1))
                evac_copy(P, 'act' if i_cur % 2 == 0 else 'dve', qkv[:, oc, b * T:(b + 1) * T], qkv.name, bank, T)
                if oc == 8 and b == NBLK - 1 and hh <= 2 and getattr(G, 'pump', None) is not None:
                    G.pump(1)

            def transp(g, q):
                dil, nb = DIL[g]
                for q4 in range(4):
                    bi = q * 4 + q4
                    r, kb = bi // nb, bi % nb
                    start = r + dil * 128 * kb
                    ks = slice(start, start + dil * 127 + 1, dil)
                    P.op('pe', lambda e: e.transpose(out=psT[:, q4 * 128:(q4 + 1) * 128], in_=qkv[:, 6 + g, ks],
                                                     identity=G.ident_b[:, :]),
                         reads=[qkv.name, 'ident_b'], writes=['ps2'], signal=(q4 == 3))
                P.op('dve', lambda e: e.tensor_copy(
                    out=vtm[:, g, q * 4:q * 4 + 4, :].rearrange('p a b -> p (a b)'), in_=psT[:, 0:512]),
                    reads=['ps2'], writes=[vtm.name])

            for i_ in range(len(sched)):
                jobs.append(lambda i_=i_: group(i_))
            for g in range(3):
                for q in range(4):
                    jobs.append(lambda g=g, q=q: transp(g, q))
            return jobs

        def stage_b(k):
            s_, hh = items[k]
            tb = s_ * S
            qkv, vtm = qkv2[k % 2], vtm2[k % 2]
            blocks = [(g, bi) for g in range(3) for bi in range(16)]
            nblk = len(blocks)

            def geom(g, bi):
                dil, nb = DIL[g]
                r, kb = bi // nb, bi % nb
                start = r + dil * 128 * kb
                nq = 128 if kb == nb - 1 else 256
                return slice(start, start + dil * 127 + 1, dil), slice(start, start + dil * (nq - 1) + 1, dil), nq

            def emit_scores(idx):
                g, bi = blocks[idx]
                ks, qs, nq = geom(g, bi)
                bank = 3 + idx % 3
                P.op('pe', lambda e: e.matmul(G.ps[bank][:, 0:nq], lhsT=qkv[:, 3 + g, ks], rhs=qkv[:, g, qs],
                                              start=True, stop=True),
                     reads=[qkv.name], writes=['ps%d' % bank], signal=True)

            def emit_exp(idx):
                g, bi = blocks[idx]
                ks, qs, nq = geom(g, bi)
                bank = 3 + idx % 3
                et, p_t = e_t[idx % 4], pt[idx % 4]
                P.op('act', lambda e: e.activation(out=et[:, 0:nq], in_=G.ps[bank][:, 0:nq], func=AF.Exp, scale=DIL_SCALE),
                     reads=['ps%d' % bank], writes=[et.name])
                P.op('dve', lambda e: e.tensor_tensor(out=p_t[:, 0:nq], in0=et[:, 0:nq], in1=G.E[:, 4 * g + hh, 0:nq],
                                                      op=ALU.mult),
                     reads=[et.name, 'E'], writes=[p_t.name])

            def emit_pv(idx):
                g, bi = blocks[idx]
                ks, qs, nq = geom(g, bi)
                p_t = pt[idx % 4]
                bo = 6 + (idx % 2)
                P.op('pe', lambda e: e.matmul(G.ps[bo][:, 0:nq], lhsT=vtm[:, g, bi, :], rhs=p_t[:, 0:nq],
                                              start=True, stop=True),
                     reads=[vtm.name, p_t.name], writes=['ps%d' % bo], signal=False)
                P.op('pe', lambda e: e.matmul(G.ps[bo][:, 256:256 + nq], lhsT=G.ones_b[:, :], rhs=p_t[:, 0:nq],
                                              start=True, stop=True),
                     reads=['ones_b', p_t.name], writes=['ps%d' % bo], signal=True)
                P.op('dve', lambda e: e.tensor_tensor(
                    out=nd[:, :, qs], in0=nd[:, :, qs],
                    in1=G.ps[bo][:, :].rearrange('p (a b) -> p a b', a=2)[:, :, 0:nq], op=ALU.add),
                    reads=['ps%d' % bo], writes=[nd.name])

            def step(t):
                if t == -1:
                    P.op('dve', lambda e: e.memset(nd[:, :, :], 0.0), writes=[nd.name])
                    emit_scores(0)
                    emit_scores(1)
                    emit_exp(0)
                    return
                if t + 2 < nblk:
                    emit_scores(t + 2)
                if t + 1 < nblk:
                    emit_exp(t + 1)
                emit_pv(t)

            def norm(b):
                cs = slice(b * T, (b + 1) * T)
                o_t = oc_t[b % 2]
                P.op('dve', lambda e: e.reciprocal(out=rec[:, :], in_=nd[:, 1, cs]), reads=[nd.name], writes=[rec.name])
                P.op('dve', lambda e: e.tensor_tensor(out=o_t[:, :], in0=nd[:, 0, cs], in1=rec[:, :], op=ALU.mult),
                     reads=[nd.name, rec.name], writes=[o_t.name])
                t0 = tb + b * T
                P.dma('pool', out=G.OC[hh * 128:(hh + 1) * 128, t0:t0 + T], in_=o_t[:, :],
                      reads=[o_t.name], writes=['OC_%d' % (t0 // T)], acc=True)

            jobs = [lambda t=t: step(t) for t in range(-1, nblk)]
            jobs += [lambda b=b: norm(b) for b in range(NBLK)]
            return jobs

        prevB = []
        for k in range(len(items)):
            A = stage_a(k)
            na, nb_ = len(A), len(prevB)
            done_b = 0
            for i_, job in enumerate(A):
                job()
                want = (nb_ * (i_ + 1)) // na
                while done_b < want:
                    prevB.pop(0)()
                    done_b += 1
            while prevB:
                prevB.pop(0)()
            prevB = stage_b(k)
        while prevB:
            prevB.pop(0)()


def merge_phase(P, nc, G, l, NT, lnidx):
    T = 512
    NB = NT // T
    wg_w, wb_w, wo_w = G.win_gate[l], G.wbr[l], G.wout[l]
    with ExitStack() as st:
        xb = sb(st, nc, 'g_xb', [128, DC, T], BF16)
        br = sb(st, nc, 'g_br', [128, DC, T], BF16)
        mg = sb(st, nc, 'g_mg', [128, DC, T], BF16)
        z = sb(st, nc, 'g_z', [128, DC, T], F32)
        xbo = sb(st, nc, 'g_xbo', [128, DC, T], BF16)
        wg = [sb(st, nc, 'g_wg%d' % k, [128, DC, 128], BF16) for k in range(6)]
        wb = [sb(st, nc, 'g_wb%d' % k, [128, DC, 128], BF16) for k in range(2)]
        wo = [sb(st, nc, 'g_wo%d' % k, [128, DC, 128], BF16) for k in range(3)]
        sig = [sb(st, nc, 'g_sig%d' % k, [128, T], F32) for k in range(3)]
        ta = sb(st, nc, 'g_ta', [128, T], F32)
        tb_ = sb(st, nc, 'g_tb', [128, T], F32)
        xc = [sb(st, nc, 'g_xc%d' % k, [128, T], F32) for k in range(4)]
        zsq = [sb(st, nc, 'g_zsq%d' % k, [128, T], F32) for k in range(2)]
        mean_t = sb(st, nc, 'g_mean', [128, T], F32)
        rstd_t = sb(st, nc, 'g_rstd', [128, T], F32)
        tmp_t = sb(st, nc, 'g_tmp', [128, T], F32)
        acc1 = sb(st, nc, 'g_acc1', [128, T], F32)
        acc2 = sb(st, nc, 'g_acc2', [128, T], F32)

        def load_gate(b, c):
            for bi in range(3):
                w_ = wg[((b * DC + c) % 2) * 3 + bi]
                ti = bi * DC + c
                P.dma('sp', out=w_[:, :, :], in_=wg_w.tile(ti).rearrange('p (k n) -> p k n', n=128),
                      reads=[wg_w.key(ti)], writes=[w_.name])
            w_ = wb[(b * DC + c) % 2]
            P.dma('sp', out=w_[:, :, :], in_=wb_w.tile(c).rearrange('p (k n) -> p k n', n=128),
                  reads=[wb_w.key(c)], writes=[w_.name])

        def load_wo(b, c):
            w_ = wo[(b * DC + c) % 3]
            P.dma('sp', out=w_[:, :, :], in_=wo_w.tile(c).rearrange('p (k n) -> p k n', n=128),
                  reads=[wo_w.key(c)], writes=[w_.name])

        pending = []
        for b in range(NB):
            t0 = b * T
            P.dma('sp', out=xb[:, :, :], in_=G.XBv[:, :, t0:t0 + T], reads=['XB_%d' % b], writes=[xb.name])
            P.dma('sp', out=br[:, 0:8, :], in_=G.OAv[:, :, t0:t0 + T], reads=['OA_%d' % b], writes=[br.name])
            P.dma('sp', out=br[:, 8:12, :], in_=G.OBv[:, :, t0:t0 + T], reads=['OB_%d' % b], writes=[br.name], acc=True)
            P.dma('sp', out=br[:, 12:16, :], in_=G.OCv[:, :, t0:t0 + T], reads=['OC_%d' % b], writes=[br.name], acc=True)
            if b == 0:
                load_gate(b, 0)
            for c in range(DC):
                if c + 1 < DC:
                    load_gate(b, c + 1)
                elif c + 1 == DC:
                    load_wo(b, 0)
                    load_wo(b, 1)
                par = (b * DC + c) % 2
                for bi in range(3):
                    w_ = wg[par * 3 + bi]
                    for kc in range(DC):
                        P.op('pe', lambda e: e.matmul(G.ps[bi][:, 0:T], lhsT=w_[:, kc, :], rhs=xb[:, kc, :],
                                                      start=(kc == 0), stop=(kc == DC - 1)),
                             reads=[w_.name, xb.name], writes=['ps%d' % bi], signal=(kc == DC - 1))
                w_ = wb[par]
                for bi, (k0, k1) in enumerate(((0, 8), (8, 12), (12, 16))):
                    for kc in range(k0, k1):
                        P.op('pe', lambda e: e.matmul(G.ps[3 + bi][:, 0:T], lhsT=w_[:, kc, :], rhs=br[:, kc, :],
                                                      start=(kc == k0), stop=(kc == k1 - 1)),
                             reads=[w_.name, br.name], writes=['ps%d' % (3 + bi)], signal=(kc == k1 - 1))
                for bi in range(3):
                    P.op('act', lambda e: e.activation(out=sig[bi][:, :], in_=G.ps[bi][:, 0:T], func=AF.Sigmoid),
                         reads=['ps%d' % bi], writes=[sig[bi].name])
                P.op('dve', lambda e: e.tensor_tensor(out=ta[:, :], in0=G.ps[3][:, 0:T], in1=sig[0][:, :], op=ALU.mult),
                     reads=['ps3', sig[0].name], writes=[ta.name])
                P.op('dve', lambda e: e.tensor_tensor(out=tb_[:, :], in0=G.ps[4][:, 0:T], in1=sig[1][:, :], op=ALU.mult),
                     reads=['ps4', sig[1].name], writes=[tb_.name])
                P.op('dve', lambda e: e.tensor_tensor(out=ta[:, :], in0=ta[:, :], in1=tb_[:, :], op=ALU.add),
                     reads=[tb_.name], writes=[ta.name])
                P.op('dve', lambda e: e.tensor_tensor(out=tb_[:, :], in0=G.ps[5][:, 0:T], in1=sig[2][:, :], op=ALU.mult),
                     reads=['ps5', sig[2].name], writes=[tb_.name])
                P.op('dve', lambda e: e.tensor_tensor(out=mg[:, c, :], in0=ta[:, :], in1=tb_[:, :], op=ALU.add),
                     reads=[ta.name, tb_.name], writes=[mg.name])
                if pending:
                    pending.pop(0)()
            while pending:
                pending.pop(0)()
            for c in range(DC):
                if c + 2 < DC:
                    load_wo(b, c + 2)
                elif c + 2 == DC and b + 1 < NB:
                    load_gate(b + 1, 0)
                xct = xc[c % 4]
                P.dma('sp', out=xct[:, :], in_=G.XTv[:, c, t0:t0 + T], reads=['XT_%d' % b], writes=[xct.name])
                w_ = wo[(b * DC + c) % 3]
                bo = c % 2
                for kc in range(DC):
                    P.op('pe', lambda e: e.matmul(G.ps[bo][:, 0:T], lhsT=w_[:, kc, :], rhs=mg[:, kc, :],
                                                  start=(kc == 0), stop=(kc == DC - 1)),
                         reads=[w_.name, mg.name], writes=['ps%d' % bo], signal=(kc == DC - 1))
                P.op('dve', lambda e: e.scalar_tensor_tensor(out=z[:, c, :], in0=xct[:, :], scalar=ALPHA,
                                                             in1=G.ps[bo][:, 0:T], op0=ALU.mult, op1=ALU.add),
                     reads=[xct.name, 'ps%d' % bo], writes=[zk(z, c)])
                ln_stats_chunk(P, G, z[:, c, :], zk(z, c), zsq[c % 2], zsq[c % 2].name, c, DC, T, 6, 7, acc1, acc2)
            ln_finish_stats(P, G, T, D, LN_EPS, 6, 7, mean_t, rstd_t, tmp_t)
            G.pump_ok = (b + 1 < NB)
            pending = ln_apply_jobs(P, G, z, T, lnidx, mean_t, rstd_t, xbo,
                                    G.XTv[:, :, t0:t0 + T], G.XBv[:, :, t0:t0 + T], 'XT_%d' % b, 'XB_%d' % b)
        while pending:
            pending.pop(0)()


XA_SCALE = 128 ** -0.5


def xattn_phase(P, nc, G, l, seqs, lnidx):
    T = 512
    NBLK = S // T
    wq_w, wkv_w, wo_w = G.xa_wq[l], G.xa_wkv[l], G.xa_wo[l]
    with ExitStack() as st:
        memT = sb(st, nc, 'x_memT', [128, DC, MEM], BF16)
        wt = [sb(st, nc, 'x_wt%d' % k, [128, DC, 128], BF16) for k in range(3)]
        wq = [sb(st, nc, 'x_wq%d' % k, [128, DC, 128], BF16) for k in range(4)]
        wo = [sb(st, nc, 'x_wo%d' % k, [128, 4, 128], BF16) for k in range(16)]
        kx = sb(st, nc, 'x_kx', [128, 4, MEM], BF16)
        vx = sb(st, nc, 'x_vx', [128, 4, 2, 128], BF16)
        xb = [sb(st, nc, 'x_xb%d' % k, [128, DC, T], BF16) for k in range(2)]
        qx = sb(st, nc, 'x_qx', [128, 4, T], BF16)
        ox = sb(st, nc, 'x_ox', [128, 4, T], BF16)
        pt = [sb(st, nc, 'x_pt%d' % k, [128, T], BF16) for k in range(2)]
        rec = sb(st, nc, 'x_rec', [128, T], F32)
        z = sb(st, nc, 'x_z', [128, DC, T], F32)
        xbo = sb(st, nc, 'x_xbo', [128, DC, T], BF16)
        xc = [sb(st, nc, 'x_xc%d' % k, [128, T], F32) for k in range(4)]
        zsq = [sb(st, nc, 'x_zsq%d' % k, [128, T], F32) for k in range(4)]
        mean_t = sb(st, nc, 'x_mean', [128, T], F32)
        rstd_t = sb(st, nc, 'x_rstd', [128, T], F32)
        tmp_t = sb(st, nc, 'x_tmp', [128, T], F32)
        acc1 = sb(st, nc, 'x_acc1', [128, T], F32)
        acc2 = sb(st, nc, 'x_acc2', [128, T], F32)

        for h in range(4):
            P.dma('sp', out=wq[h][:, :, :], in_=wq_w.tile(h).rearrange('p (k n) -> p k n', n=128),
                  reads=[wq_w.key(h)], writes=[wq[h].name])
        for c in range(16):
            P.dma('sp', out=wo[c][:, :, :], in_=wo_w.tile(c).rearrange('p (k n) -> p k n', n=128),
                  reads=[wo_w.key(c)], writes=[wo[c].name])
        for s in seqs:
            tb = s * S
            P.dma('sp', out=memT[:, :, :], in_=G.MEMBv[:, :, s * MEM:(s + 1) * MEM], reads=['MEMB'], writes=[memT.name])
            for i in range(8):
                w_ = wt[i % 3]
                P.dma('sp', out=w_[:, :, :], in_=wkv_w.tile(i).rearrange('p (k n) -> p k n', n=128),
                      reads=[wkv_w.key(i)], writes=[w_.name])
                if i < 4:
                    for kc in range(DC):
                        P.op('pe', lambda e: e.matmul(G.ps[0][:, 0:MEM], lhsT=w_[:, kc, :], rhs=memT[:, kc, :],
                                                      start=(kc == 0), stop=(kc == DC - 1)),
                             reads=[w_.name, memT.name], writes=['ps0'], signal=(kc == DC - 1))
                    evac_copy(P, 'act', kx[:, i, :], kx.name, 0, MEM)
                else:
                    h = i - 4
                    for mt in range(2):
                        for kc in range(DC):
                            P.op('pe', lambda e: e.matmul(G.ps[1][:, mt * 128:(mt + 1) * 128],
                                                          lhsT=memT[:, kc, mt * 128:(mt + 1) * 128], rhs=w_[:, kc, :],
                                                          start=(kc == 0), stop=(kc == DC - 1)),
                                 reads=[w_.name, memT.name], writes=['ps1'], signal=(mt == 1 and kc == DC - 1))
                    evac_copy(P, 'act', vx[:, h, :, :].rearrange('p a b -> p (a b)'), vx.name, 1, 256)
            P.dma('sp', out=xb[0][:, :, :], in_=G.XBv[:, :, tb:tb + T], reads=['XB_%d' % (tb // T)], writes=[xb[0].name])
            pending = []
            for b in range(NBLK):
                xbt = xb[b % 2]
                t0 = tb + b * T
                gb = t0 // T
                if b + 1 < NBLK:
                    t1 = t0 + T
                    P.dma('sp', out=xb[(b + 1) % 2][:, :, :], in_=G.XBv[:, :, t1:t1 + T], reads=['XB_%d' % (t1 // T)],
                          writes=[xb[(b + 1) % 2].name])
                for h in range(4):
                    bank = h % 2
                    for kc in range(DC):
                        P.op('pe', lambda e: e.matmul(G.ps[bank][:, 0:T], lhsT=wq[h][:, kc, :], rhs=xbt[:, kc, :],
                                                      start=(kc == 0), stop=(kc == DC - 1)),
                             reads=[wq[h].name, xbt.name], writes=['ps%d' % bank], signal=(kc == DC - 1))
                    evac_copy(P, 'act', qx[:, h, :], qx.name, bank, T)
                    for _ in range(2):
                        if pending:
                            pending.pop(0)()
                for h in range(4):
                    for _ in range(3):
                        if pending:
                            pending.pop(0)()
                    bo, bl = 4 + (h % 2), 6 + (h % 2)
                    for mt in range(2):
                        bank = 2 + mt
                        P.op('pe', lambda e: e.matmul(G.ps[bank][:, 0:T], lhsT=kx[:, h, mt * 128:(mt + 1) * 128], rhs=qx[:, h, :],
                                                      start=True, stop=True),
                             reads=[kx.name, qx.name], writes=['ps%d' % bank], signal=True)
                    for mt in range(2):
                        bank = 2 + mt
                        p_t = pt[mt]
                        P.op('act', lambda e: e.activation(out=p_t[:, :], in_=G.ps[bank][:, 0:T], func=AF.Exp, scale=XA_SCALE),
                             reads=['ps%d' % bank], writes=[p_t.name])
                        P.op('pe', lambda e: e.matmul(G.ps[bo][:, 0:T], lhsT=vx[:, h, mt, :], rhs=p_t[:, :],
                                                      start=(mt == 0), stop=(mt == 1)),
                             reads=[vx.name, p_t.name], writes=['ps%d' % bo], signal=(mt == 1))
                        P.op('pe', lambda e: e.matmul(G.ps[bl][:, 0:T], lhsT=G.ones_b[:, :], rhs=p_t[:, :],
                                                      start=(mt == 0), stop=(mt == 1)),
                             reads=['ones_b', p_t.name], writes=['ps%d' % bl], signal=True)
                    P.op('dve', lambda e: e.reciprocal(out=rec[:, :], in_=G.ps[bl][:, 0:T]), reads=['ps%d' % bl], writes=[rec.name])
                    P.op('dve', lambda e: e.tensor_tensor(out=ox[:, h, :], in0=G.ps[bo][:, 0:T], in1=rec[:, :], op=ALU.mult),
                         reads=['ps%d' % bo, rec.name], writes=[ox.name])
                while pending:
                    pending.pop(0)()
                for c in range(DC):
                    xct = xc[c % 4]
                    P.dma('sp', out=xct[:, :], in_=G.XTv[:, c, t0:t0 + T], reads=['XT_%d' % gb], writes=[xct.name])
                    bo = c % 2
                    for kc in range(4):
                        P.op('pe', lambda e: e.matmul(G.ps[bo][:, 0:T], lhsT=wo[c][:, kc, :], rhs=ox[:, kc, :],
                                                      start=(kc == 0), stop=(kc == 3)),
                             reads=[wo[c].name, ox.name], writes=['ps%d' % bo], signal=(kc == 3))
                    P.op('dve', lambda e: e.scalar_tensor_tensor(out=z[:, c, :], in0=xct[:, :], scalar=ALPHA,
                                                                 in1=G.ps[bo][:, 0:T], op0=ALU.mult, op1=ALU.add),
                         reads=[xct.name, 'ps%d' % bo], writes=[zk(z, c)])
                    ln_stats_chunk(P, G, z[:, c, :], zk(z, c), zsq[c % 4], zsq[c % 4].name, c, DC, T, 6, 7, acc1, acc2, mode='dve')
                ln_finish_stats(P, G, T, D, LN_EPS, 6, 7, mean_t, rstd_t, tmp_t)
                G.pump_ok = (b + 1 < NBLK)
                pending = ln_apply_jobs(P, G, z, T, lnidx, mean_t, rstd_t, xbo,
                                        G.XTv[:, :, t0:t0 + T], G.XBv[:, :, t0:t0 + T], 'XT_%d' % gb, 'XB_%d' % gb)
            while pending:
                pending.pop(0)()


WSPECS = [
    ('win_mla', 9, 2048, 9), ('win_u', 4, 2048, 4), ('win_v', 1, 8192, 1), ('win_qkv', 36, 2048, 9),
    ('win_gate', 48, 2048, 8), ('wuq', 16, 512, 16), ('wukv', 16, 512, 16), ('wbr', 16, 2048, 8),
    ('wout', 16, 2048, 8), ('xa_wq', 4, 2048, 4), ('xa_wkv', 8, 2048, 8), ('xa_wo', 16, 512, 16),
]


def e_table_jobs(P, nc, G, st, bank=5):
    rel = sb(st, nc, 'e_rel', [33, 12], F32)
    oh = sb(st, nc, 'e_oh', [33, 3, 385], F32)
    tmp = [sb(st, nc, 'e_tmp%d' % k, [33, 385], F32) for k in range(2)]
    trow = [sb(st, nc, 'e_trow%d' % k, [128, 385], F32) for k in range(2)]

    def head(h):
        if h == 0:
            P.dma('sp', out=rel[:, :], in_=G.rel_d, writes=[rel.name])
            P.dma('sp', out=oh[:, :, :], in_=G.oh_d, writes=[oh.name])
        g = h // 4
        t_, r_ = tmp[h % 2], trow[h % 2]
        P.op('dve', lambda e: e.tensor_scalar(out=t_[:, :], in0=oh[:, g, :], scalar1=rel[:, h:h + 1], scalar2=None,
                                              op0=ALU.mult),
             reads=[rel.name, oh.name], writes=[t_.name])
        P.op('pe', lambda e: e.matmul(G.ps[bank][:, 0:385], lhsT=G.ones_f[0:33, :], rhs=t_[:, :], start=True, stop=True),
             reads=[t_.name, 'ones_f'], writes=['ps%d' % bank])
        P.op('act', lambda e: e.activation(out=r_[:, :], in_=G.ps[bank][:, 0:385], func=AF.Exp),
             reads=['ps%d' % bank], writes=[r_.name])
        P.dma('pool', out=G.gm_d[h].rearrange('(k c) -> k c', c=385), in_=r_[:, :], reads=[r_.name], writes=['gm%d' % h])
        P.dma('pool', out=G.E_d[:, h, :],
              in_=G.gm_d[h][127:127 + 128 * 384].rearrange('(k c) -> k c', c=384)[:, 0:256],
              reads=['gm%d' % h], writes=['E_d'], acc=(h > 0))

    return [lambda h=h: head(h) for h in range(12)]


def build_program(NSEQ=2, NLAYERS=2, upto=None, dbg=False):
    NT = NSEQ * S
    nc = bass.Bass("TRN2", target_bir_lowering=False)
    G = Ctx()
    G.NT = NT

    def din(name, shape, dt=F32):
        return nc.dram_tensor(name, list(shape), dt, kind='ExternalInput').ap()

    def dint(name, shape, dt):
        return nc.dram_tensor(name, list(shape), dt, kind='Internal').ap()

    xT = din('xT', [D, NT])
    memT = din('memT', [D, NSEQ * MEM])
    G.lng_d = din('lng', [128, 8, DC])
    G.lnb_d = din('lnb', [128, 8, DC])
    G.wgu, G.wd = {}, {}
    for l in range(NLAYERS):
        for i in range(2):
            G.wgu[(l, i)] = WT(nc, 'wgu%d%d' % (l, i), FC, DC * 256, 4)
            G.wd[(l, i)] = WT(nc, 'wd%d%d' % (l, i), DC, FC * 128, 2)
    for name, n, F, grp in WSPECS:
        setattr(G, name, [WT(nc, '%s%d' % (name, l), n, F, grp) for l in range(NLAYERS)])
    qnorm_d = din('qnorm', [128, DEPTH, 4])
    kvnorm_d = din('kvnorm', [128, DEPTH, 4])
    G.sgu_g_d = din('sgu_g', [DEPTH, 512])
    G.sgu_b_d = din('sgu_b', [DEPTH, 512])
    G.wsT_d = din('wsT', [DEPTH, 128, 4, 128])
    G.bs_d = din('bs', [DEPTH, 4, 128])
    G.rel_d = din('rel33', [33, 12])
    G.oh_d = din('oh', [33, 3, 385])
    tril_d = din('tril', [128, 128])
    ident_d = din('ident', [128, 128])
    G.cos_d = din('cosT', [64, S])
    G.sin_d = din('sinT', [64, S])

    XT = dint('XT', [D, NT], F32)
    XB = dint('XB', [D, NT], BF16)
    G.OA = dint('OA', [1024, NT], BF16)
    OB = dint('OB', [512, NT], BF16)
    G.OC = dint('OC', [512, NT], BF16)
    MEMB = dint('MEMB', [D, NSEQ * MEM], BF16)
    G.gm_d = dint('gm', [12, 128 * 385], F32)
    G.E_d = dint('E_d', [128, 12, 256], F32)
    yT = nc.dram_tensor('yT', [D, NT], F32, kind='ExternalOutput').ap()

    fm = lambda a: a.rearrange('(c p) t -> p c t', p=128)
    G.XTv, G.XBv, G.OAv, G.OBv, G.OCv, G.MEMBv = fm(XT), fm(XB), fm(G.OA), fm(OB), fm(G.OC), fm(MEMB)
    xTv, yTv = fm(xT), fm(yT)

    with ExitStack() as st:
        P = Prog(nc, st)
        P.G = G
        build_globals(P, nc, G, st)
        G.qnorm = sb(st, nc, 'qnorm_sb', [128, DEPTH, 4], F32)
        G.kvnorm = sb(st, nc, 'kvnorm_sb', [128, DEPTH, 4], F32)
        G.tri_f = sb(st, nc, 'tri_f', [128, 128], F32)
        G.tri_b = sb(st, nc, 'tri_b', [128, 128], BF16)
        G.ident_b = sb(st, nc, 'ident_b', [128, 128], BF16)
        P.dma('sp', out=G.qnorm[:, :, :], in_=qnorm_d, writes=['smallp'])
        P.dma('sp', out=G.kvnorm[:, :, :], in_=kvnorm_d, writes=['smallp'], acc=True)
        P.dma('sp', out=G.tri_f[:, :], in_=tril_d, writes=['tri_f'])
        P.dma('pool', out=G.tri_b[:, :], in_=tril_d, writes=['tri_b'])
        P.dma('pool', out=G.ident_b[:, :], in_=ident_d, writes=['ident_b'])
        nblk = NT // 512

        def xb_job(b):
            return lambda: P.dma('poolc', out=XB[:, b * 512:(b + 1) * 512], in_=xT[:, b * 512:(b + 1) * 512],
                                 writes=['XB_%d' % b])

        castq = [xb_job(0)]
        castq += G.wgu[(0, 0)].jobs()
        castq += G.wd[(0, 0)].jobs()
        for b in range(1, nblk):
            castq.append(xb_job(b))
        castq.append(lambda: P.dma('poolc', out=MEMB[:, :], in_=memT[:, :], writes=['MEMB']))
        n_start = len(castq)
        for l in range(NLAYERS):
            if l > 0:
                castq += G.wgu[(l, 0)].jobs() + G.wd[(l, 0)].jobs()
            for name, _, _, _ in WSPECS:
                castq += getattr(G, name)[l].jobs()
            castq += G.wgu[(l, 1)].jobs() + G.wd[(l, 1)].jobs()

        def pump(n):
            for _ in range(n):
                if castq:
                    j = castq.pop(0)
                    if callable(j):
                        j()
                    else:
                        j[0].issue(P, j[1])

        def need(*ws):
            for w in ws:
                while not w.done:
                    assert castq
                    pump(1)

        G.pump = pump
        pump(n_start)
        G.extra_jobs = e_table_jobs(P, nc, G, st)
        pump(4)

        stop = False
        for l in range(NLAYERS):
            last = (l == NLAYERS - 1)
            phases = ['ffn0', 'mix', 'merge', 'xattn', 'ffn1']
            for pi, ph in enumerate(phases):
                if ph == 'ffn0':
                    need(G.wgu[(l, 0)], G.wd[(l, 0)])
                    ffn_phase(P, nc, G, l, 0, xTv if l == 0 else G.XTv, G.XTv, G.XBv, NT, l * 4 + 0)
                    P.barrier()
                elif ph == 'mix':
                    need(G.win_mla[l], G.win_u[l], G.win_v[l], G.win_qkv[l], G.wuq[l], G.wukv[l])
                    while getattr(G, 'extra_jobs', None):
                        G.extra_jobs.pop(0)()
                    mla_phase(P, nc, G, l, list(range(NSEQ)))
                    P.barrier()
                    sgu_phase(P, nc, G, l, list(range(NSEQ)))
                    P.barrier()
                    dil_phase(P, nc, G, l, list(range(NSEQ)))
                    P.barrier()
                elif ph == 'merge':
                    need(G.win_gate[l], G.wbr[l], G.wout[l])
                    merge_phase(P, nc, G, l, NT, l * 4 + 1)
                    P.barrier()
                elif ph == 'xattn':
                    need(G.xa_wq[l], G.xa_wkv[l], G.xa_wo[l])
                    xattn_phase(P, nc, G, l, list(range(NSEQ)), l * 4 + 2)
                    P.barrier()
                elif ph == 'ffn1':
                    need(G.wgu[(l, 1)], G.wd[(l, 1)])
                    ffn_phase(P, nc, G, l, 1, G.XTv, yTv if (last and upto is None) else G.XTv,
                              None if (last and upto is None) else G.XBv, NT, l * 4 + 3)
                    P.barrier()
                if upto is not None and (l, pi) == tuple(upto):
                    stop = True
                    break
            if stop:
                break
        if upto is not None and dbg:
            for nm, src in (('dOA', G.OA), ('dOB', OB), ('dOC', G.OC)):
                dd = nc.dram_tensor(nm, list(src.shape), BF16, kind='ExternalOutput').ap()
                nb_ = NT // 512
                P.dma('sp', out=dd[:, :], in_=src[:, :], reads=['%s_%d' % (nm[1:], b_) for b_ in range(nb_)], writes=[nm])
        if upto is not None:
            with nc.sbuf_tensor('dump', [128, DC, 512], F32) as dump:
                for b in range(NT // 512):
                    P.dma('sp', out=dump[:, :, :], in_=G.XTv[:, :, b * 512:(b + 1) * 512], reads=['XT_%d' % b], writes=['dump'])
                    P.dma('sp', out=yTv[:, :, b * 512:(b + 1) * 512], in_=dump[:, :, :], reads=['dump'], writes=['yT_%d' % b])
                P.finish()
        else:
            P.finish()
        G.ninstr = P.nins
    return nc, G


def _t5_bucket_np(dist):
    import math
    exact = 16
    df = np.maximum(dist, 1).astype(np.float32)
    large = exact + (np.log(df / np.float32(exact)) / np.float32(math.log(2048 / exact)) * np.float32(32 - exact)).astype(np.int32)
    large = np.minimum(large, 31)
    return np.where(dist < exact, dist, large)


def host_constants():
    c = {}
    oh = np.zeros((33, 3, 385), np.float32)
    for g, dil in enumerate((1, 4, 16)):
        for m in range(385):
            j = m - 127
            if 0 <= j <= 128:
                oh[int(_t5_bucket_np(np.array(j * dil))), g, m] = 1.0
            else:
                oh[32, g, m] = 1.0
    c['oh'] = oh
    k = np.arange(128)
    c['tril'] = (k[:, None] <= k[None, :]).astype(np.float32)
    c['ident'] = np.eye(128, dtype=np.float32)
    inv = (np.float32(10000.0) ** (-np.arange(0, 64, 2, dtype=np.float32) / np.float32(64))).astype(np.float32)
    ang = (np.arange(S, dtype=np.float32)[:, None] * inv[None, :]).astype(np.float32)
    cs, sn = np.cos(ang).astype(np.float32).T, np.sin(ang).astype(np.float32).T
    c['cosT'] = np.ascontiguousarray(np.concatenate([cs, cs], 0))
    c['sinT'] = np.ascontiguousarray(np.concatenate([-sn, sn], 0))
    return c


def host_weights(inp, layers):
    w = {}
    A = lambda a: np.asarray(a, dtype=np.float32)
    sw = np.concatenate([np.arange(32, 64), np.arange(0, 32)])
    for l in layers:
        for i in range(2):
            w['wgu%d%d' % (l, i)] = tile_wgu(A(inp['ffn_wg'][l, i]), A(inp['ffn_wu'][l, i]))
            w['wd%d%d' % (l, i)] = tile_w(A(inp['ffn_wd'][l, i]))
        win = A(inp['w_in'][l])
        kpe = win[:, 1024:1088]
        w['win_mla%d' % l] = tile_w(np.concatenate([win[:, 0:1024], kpe, kpe[:, sw]], 1))
        w['win_u%d' % l] = tile_w(win[:, 1088:1600])
        w['win_v%d' % l] = np.ascontiguousarray(win[:, 1600:2112].reshape(16, 128, 512).transpose(1, 0, 2)).reshape(1, 128, 8192)
        cols = []
        for hh in range(4):
            for t in range(3):
                for g in range(3):
                    h = 4 * g + hh
                    cols.append(np.arange(2112 + t * 1536 + h * 128, 2112 + t * 1536 + (h + 1) * 128))
        w['win_qkv%d' % l] = tile_w(win[:, np.concatenate(cols)])
        w['win_gate%d' % l] = tile_w(win[:, 6720:])
        uq = A(inp['mla_w_uq'][l])
        cols = []
        for h in range(8):
            base = h * 192
            cols += [np.arange(base, base + 128), np.arange(base + 128, base + 192), base + 128 + sw]
        w['wuq%d' % l] = tile_w(uq[:, np.concatenate(cols)])
        w['wukv%d' % l] = tile_w(A(inp['mla_w_ukv'][l]))
        w['wbr%d' % l] = tile_w(A(inp['w_branch'][l]))
        w['wout%d' % l] = tile_w(A(inp['w_out'][l]))
        w['xa_wq%d' % l] = tile_w(A(inp['xa_wq'][l]))
        w['xa_wkv%d' % l] = tile_w(A(inp['xa_wkv'][l]))
        w['xa_wo%d' % l] = tile_w(A(inp['xa_wo'][l]))
    L = DEPTH
    w['lng'] = np.ascontiguousarray(A(inp['ln_g']).reshape(L * 4, DC, 128).transpose(2, 0, 1))
    w['lnb'] = np.ascontiguousarray(A(inp['ln_b']).reshape(L * 4, DC, 128).transpose(2, 0, 1))
    w['qnorm'] = np.ascontiguousarray(A(inp['mla_q_norm']).reshape(L, 4, 128).transpose(2, 0, 1))
    w['kvnorm'] = np.ascontiguousarray(A(inp['mla_kv_norm']).reshape(L, 4, 128).transpose(2, 0, 1))
    w['sgu_g'] = np.ascontiguousarray(A(inp['sgu_ln_g']))
    w['sgu_b'] = np.ascontiguousarray(A(inp['sgu_ln_b']))
    w['wsT'] = np.ascontiguousarray(A(inp['sgu_ws']).transpose(0, 3, 1, 2))
    w['bs'] = np.ascontiguousarray(A(inp['sgu_bs']))
    w['rel33'] = np.ascontiguousarray(np.concatenate([A(inp['rel_bias']), np.full((1, 12), -30000.0, np.float32)], 0))
    w.update(host_constants())
    return w


_CACHE = {}


def kernel(**inputs):
    NCORES = 8
    NSEQ = 16 // NCORES
    if 'nc' not in _CACHE:
        _CACHE['nc'] = build_program(NSEQ=NSEQ, NLAYERS=DEPTH)[0]
    nc = _CACHE['nc']
    shared = host_weights(inputs, range(DEPTH))
    x = np.asarray(inputs['x'], dtype=np.float32)
    mem = np.asarray(inputs['mem'], dtype=np.float32)
    in_maps = []
    for c in range(NCORES):
        m = dict(shared)
        m['xT'] = np.ascontiguousarray(x[c * NSEQ:(c + 1) * NSEQ].reshape(NSEQ * S, D).T)
        m['memT'] = np.ascontiguousarray(mem[c * NSEQ:(c + 1) * NSEQ].reshape(NSEQ * MEM, D).T)
        in_maps.append(m)
    res = run_bass_kernel_spmd(nc, in_maps, core_ids=list(range(NCORES)))
    out = np.empty((16, S, D), np.float32)
    for c in range(NCORES):
        out[c * NSEQ:(c + 1) * NSEQ] = res.results[c]['yT'].T.reshape(NSEQ, S, D)
    return out
```

```python
from contextlib import ExitStack
import numpy as np
import concourse.bass as bass
import concourse.mybir as mybir
from concourse.bass_utils import run_bass_kernel_spmd

F32 = mybir.dt.float32
BF16 = mybir.dt.bfloat16
AF = mybir.ActivationFunctionType
ALU = mybir.AluOpType
AX = mybir.AxisListType

D = 2048
DC = 16
S = 2048
DEPTH = 2
MEM = 256
DFF = 5632
FC = 44
ALPHA = (2 * DEPTH) ** 0.25
LN_EPS = 1e-5
RMS_EPS = 1e-6
NIN = 12864


class Prog:
    KD = 8

    def __init__(self, nc, stack):
        self.nc = nc
        self.E = {'pe': nc.tensor, 'act': nc.scalar, 'dve': nc.vector, 'pool': nc.gpsimd, 'sp': nc.sync}
        self.sem = {k: stack.enter_context(nc.semaphore('s_' + k)) for k in ('pe', 'act', 'dve', 'pool')}
        self.cnt = {k: 0 for k in self.sem}
        self.dsem = {q: [stack.enter_context(nc.semaphore('d_%s_%d' % (q, i))) for i in range(self.KD)]
                     for q in ('sp', 'pool', 'poolc')}
        self.qeng = {'sp': 'sp', 'pool': 'pool', 'poolc': 'pool'}
        self.dcnt = {q: 0 for q in self.dsem}
        self.state = {}
        self.waited = {}
        self.nins = 0

    def _wait(self, eng, dep, force=False):
        sem, val, peng = dep
        if peng == eng and not force:
            return
        if peng is not None:
            assert self.cnt[peng] >= val, 'dependency on unsignaled instruction (%s)' % peng
        key = (eng, id(sem))
        if self.waited.get(key, 0) >= val:
            return
        self.E[eng].wait_ge(sem, val)
        self.waited[key] = val

    def _deps(self, reads, writes, acc=False):
        deps = []
        for k in reads:
            st = self.state.get(k)
            if st is not None:
                if st[0] is not None:
                    deps.append(st[0])
                deps.extend(st[2])
                if k.startswith('ps'):
                    deps.extend(st[1])
        for k in writes:
            st = self.state.get(k)
            if st is not None:
                if not acc:
                    if st[0] is not None:
                        deps.append(st[0])
                    deps.extend(st[2])
                deps.extend(st[1])
        return deps

    def _update(self, dep, reads, writes, acc=False):
        for k in writes:
            if acc:
                st = self.state.setdefault(k, [None, [], []])
                st[2] = [d for d in st[2] if d[0] is not dep[0]] + [dep]
            else:
                self.state[k] = [dep, [], []]
        for k in reads:
            if k in writes:
                continue
            st = self.state.setdefault(k, [None, [], []])
            if k.startswith('ps'):
                st[0] = dep
                st[1] = []
            else:
                st[1] = [d for d in st[1] if d[0] is not dep[0]] + [dep]

    def op(self, eng, fn, reads=(), writes=(), signal=True, sync_same=()):
        for d in self._deps(reads, writes):
            self._wait(eng, d)
        for k in sync_same:
            st = self.state.get(k)
            if st is not None and st[0] is not None:
                self._wait(eng, st[0], force=True)
        ins = fn(self.E[eng])
        self.nins += 1
        if signal:
            self.cnt[eng] += 1
            ins.then_inc(self.sem[eng], 1)
            val = self.cnt[eng]
        else:
            val = self.cnt[eng] + 1
        self._update((self.sem[eng], val, eng), reads, writes)
        return ins

    def dma(self, q, out, in_, reads=(), writes=(), acc=False):
        i = self.dcnt[q]
        slot, gen = i % self.KD, i // self.KD
        qe = self.qeng[q]
        if gen > 0:
            self._wait(qe, (self.dsem[q][slot], 16 * gen, None))
        for d in self._deps(reads, writes, acc):
            self._wait(qe, d)
        ins = self.E[qe].dma_start(out=out, in_=in_)
        ins.then_inc(self.dsem[q][slot], 16)
        self.nins += 1
        self.dcnt[q] = i + 1
        self._update((self.dsem[q][slot], 16 * (gen + 1), None), reads, writes, acc)
        return ins

    def barrier(self):
        for e in ('pe', 'act', 'dve', 'pool', 'sp'):
            for o in self.sem:
                if o != e and self.cnt[o] > 0:
                    self._wait(e, (self.sem[o], self.cnt[o], o))
            for q in ('sp', 'pool'):
                n = self.dcnt[q]
                for slot in range(self.KD):
                    k = (n - slot + self.KD - 1) // self.KD
                    if k > 0:
                        self._wait(e, (self.dsem[q][slot], 16 * k, None))

    def finish(self):
        for q in self.dsem:
            n = self.dcnt[q]
            for slot in range(self.KD):
                k = (n - slot + self.KD - 1) // self.KD
                if k > 0:
                    self._wait('sp', (self.dsem[q][slot], 16 * k, None))


_UID = [0]


def sb(st, nc, name, shape, dt):
    _UID[0] += 1
    return st.enter_context(nc.sbuf_tensor('%s_%d' % (name, _UID[0]), list(shape), dt))


class Ctx:
    pass


def ln_stats_chunk(P, G, z_ap, zkey, zsq_t, zsq_key, c, nchunks, T, bankS1, bankS2, acc1=None, acc2=None, mode='dve'):
    P.op('act', lambda e: e.activation(out=zsq_t[:, 0:T], in_=z_ap, func=AF.Square),
         reads=[zkey], writes=[zsq_key])
    if mode == 'pe':
        def emit(c=c, z_ap=z_ap, zkey=zkey, zsq_t=zsq_t, zsq_key=zsq_key):
            P.op('pe', lambda e: e.matmul(G.ps[bankS1][:, 0:T], lhsT=G.ones_f[:, :], rhs=z_ap,
                                          start=(c == 0), stop=(c == nchunks - 1)),
                 reads=[zkey, 'ones_f'], writes=['ps%d' % bankS1], signal=(c == nchunks - 1))
            P.op('pe', lambda e: e.matmul(G.ps[bankS2][:, 0:T], lhsT=G.ones_f[:, :], rhs=zsq_t[:, 0:T],
                                          start=(c == 0), stop=(c == nchunks - 1)),
                 reads=[zsq_key, 'ones_f'], writes=['ps%d' % bankS2], signal=True)
        prev = getattr(G, '_pend_stats', None)
        if prev is not None:
            prev()
        G._pend_stats = emit
        if c == nchunks - 1:
            emit()
            G._pend_stats = None
        return
    if c == 0:
        P.op('dve', lambda e: e.tensor_copy(out=acc1[:, 0:T], in_=z_ap), reads=[zkey], writes=[acc1.name])
        P.op('dve', lambda e: e.tensor_copy(out=acc2[:, 0:T], in_=zsq_t[:, 0:T]), reads=[zsq_key], writes=[acc2.name])
    else:
        P.op('dve', lambda e: e.tensor_tensor(out=acc1[:, 0:T], in0=acc1[:, 0:T], in1=z_ap, op=ALU.add),
             reads=[zkey], writes=[acc1.name])
        P.op('dve', lambda e: e.tensor_tensor(out=acc2[:, 0:T], in0=acc2[:, 0:T], in1=zsq_t[:, 0:T], op=ALU.add),
             reads=[zsq_key], writes=[acc2.name])
    if c == nchunks - 1:
        P.op('pe', lambda e: e.matmul(G.ps[bankS1][:, 0:T], lhsT=G.ones_f[:, :], rhs=acc1[:, 0:T], start=True, stop=True),
             reads=[acc1.name, 'ones_f'], writes=['ps%d' % bankS1], signal=True)
        P.op('pe', lambda e: e.matmul(G.ps[bankS2][:, 0:T], lhsT=G.ones_f[:, :], rhs=acc2[:, 0:T], start=True, stop=True),
             reads=[acc2.name, 'ones_f'], writes=['ps%d' % bankS2], signal=True)


def ln_finish_stats(P, G, T, nfeat, eps, bankS1, bankS2, mean_t, rstd_t, tmp_t, with_mean=True):
    inv = 1.0 / nfeat
    if with_mean:
        P.op('dve', lambda e: e.tensor_scalar(out=mean_t[:, 0:T], in0=G.ps[bankS1][:, 0:T], scalar1=inv, scalar2=None,
                                              op0=ALU.mult),
             reads=['ps%d' % bankS1], writes=[mean_t.name])
        P.op('dve', lambda e: e.tensor_tensor(out=tmp_t[:, 0:T], in0=mean_t[:, 0:T], in1=mean_t[:, 0:T], op=ALU.mult),
             reads=[mean_t.name], writes=[tmp_t.name])
        P.op('dve', lambda e: e.scalar_tensor_tensor(out=rstd_t[:, 0:T], in0=G.ps[bankS2][:, 0:T], scalar=inv,
                                                     in1=tmp_t[:, 0:T], op0=ALU.mult, op1=ALU.subtract),
             reads=['ps%d' % bankS2, tmp_t.name], writes=[rstd_t.name])
        P.op('dve', lambda e: e.tensor_scalar(out=rstd_t[:, 0:T], in0=rstd_t[:, 0:T], scalar1=eps, scalar2=None,
                                              op0=ALU.add),
             reads=[rstd_t.name], writes=[rstd_t.name])
    else:
        P.op('dve', lambda e: e.tensor_scalar(out=rstd_t[:, 0:T], in0=G.ps[bankS2][:, 0:T], scalar1=inv, scalar2=eps,
                                              op0=ALU.mult, op1=ALU.add),
             reads=['ps%d' % bankS2], writes=[rstd_t.name])
    P.op('act', lambda e: e.activation(out=rstd_t[:, 0:T], in_=rstd_t[:, 0:T], func=AF.Sqrt),
         reads=[rstd_t.name], writes=[rstd_t.name])
    P.op('dve', lambda e: e.reciprocal(out=rstd_t[:, 0:T], in_=rstd_t[:, 0:T]),
         reads=[rstd_t.name], writes=[rstd_t.name])


def zk(t, c):
    return '%s_c%d' % (t.name, c)


def ln_apply_jobs(P, G, z_t, T, lnidx, mean_t, rstd_t, xbo_t, dst_f, dst_b, keyf, keyb):
    jobs = []

    def chunk(c):
        zc = z_t[:, c, :]
        k = zk(z_t, c)
        P.op('dve', lambda e: e.tensor_tensor(out=zc, in0=zc, in1=mean_t[:, 0:T], op=ALU.subtract),
             reads=[mean_t.name], writes=[k])
        P.op('dve', lambda e: e.tensor_tensor(out=zc, in0=zc, in1=rstd_t[:, 0:T], op=ALU.mult),
             reads=[rstd_t.name], writes=[k])
        P.op('act', lambda e: e.activation(out=zc, in_=zc, func=AF.Identity,
                                           bias=G.lnb[:, lnidx, c:c + 1], scale=G.lng[:, lnidx, c:c + 1]),
             reads=['lnp'], writes=[k])
        if dst_b is not None:
            P.op('act', lambda e: e.activation(out=xbo_t[:, c, :], in_=zc, func=AF.Copy),
                 reads=[k], writes=[zk(xbo_t, c)])

    def store(c0, c1):
        P.dma('pool', out=dst_f[:, c0:c1, :], in_=z_t[:, c0:c1, :], reads=[zk(z_t, c) for c in range(c0, c1)],
              writes=[keyf], acc=(c0 > 0))
        if dst_b is not None:
            P.dma('pool', out=dst_b[:, c0:c1, :], in_=xbo_t[:, c0:c1, :], reads=[zk(xbo_t, c) for c in range(c0, c1)],
                  writes=[keyb], acc=(c0 > 0))
        if c1 == DC and getattr(G, 'pump', None) is not None and getattr(G, 'pump_ok', True):
            G.pump(2)

    for c in range(DC):
        if c % 4 == 3:
            jobs.append(lambda c=c: (chunk(c), store(c - 3, c + 1)))
        else:
            jobs.append(lambda c=c: chunk(c))
    return jobs


def ffn_phase(P, nc, G, l, i, src_f, dst_f, dst_b, NT, lnidx):
    T = 512
    NB = NT // T
    wgu_w = G.wgu[(l, i)]
    wd_w = G.wd[(l, i)]
    with ExitStack() as st:
        xb = sb(st, nc, 'f_xb', [128, DC, T], BF16)
        z = sb(st, nc, 'f_z', [128, DC, T], F32)
        hT = sb(st, nc, 'f_hT', [128, FC, T], BF16)
        xbo = sb(st, nc, 'f_xbo', [128, DC, T], BF16)
        wgu = [sb(st, nc, 'f_wgu%d' % k, [128, DC, 256], BF16) for k in range(3)]
        wd = [sb(st, nc, 'f_wd%d' % k, [128, FC, 128], BF16) for k in range(2)]
        xc = [sb(st, nc, 'f_xc%d' % k, [128, T], F32) for k in range(4)]
        sg = [sb(st, nc, 'f_sg%d' % k, [128, T], F32) for k in range(2)]
        zsq = [sb(st, nc, 'f_zsq%d' % k, [128, T], F32) for k in range(2)]
        mean_t = sb(st, nc, 'f_mean', [128, T], F32)
        rstd_t = sb(st, nc, 'f_rstd', [128, T], F32)
        tmp_t = sb(st, nc, 'f_tmp', [128, T], F32)
        acc1 = sb(st, nc, 'f_acc1', [128, T], F32)
        acc2 = sb(st, nc, 'f_acc2', [128, T], F32)

        def load_wgu(b, j):
            k = (b * FC + j) % 3
            P.dma('sp', out=wgu[k][:, :, :], in_=wgu_w.tile(j).rearrange('p (k n) -> p k n', n=256),
                  reads=[wgu_w.key(j)], writes=[wgu[k].name])

        def load_wd(b, c):
            k = (b * DC + c) % 2
            P.dma('sp', out=wd[k][:, :, :], in_=wd_w.tile(c).rearrange('p (k n) -> p k n', n=128),
                  reads=[wd_w.key(c)], writes=[wd[k].name])

        pending = []
        for b in range(NB):
            t0 = b * T
            P.dma('sp', out=xb[:, :, :], in_=G.XBv[:, :, t0:t0 + T], reads=['XB_%d' % b], writes=[xb.name])
            if b == 0:
                load_wgu(b, 0)
                load_wgu(b, 1)
            for j in range(FC):
                if j + 2 < FC:
                    load_wgu(b, j + 2)
                elif j + 2 == FC:
                    load_wd(b, 0)
                wt = wgu[(b * FC + j) % 3]
                bg, bu = (0, 1) if j % 2 == 0 else (2, 3)
                for kc in range(DC):
                    P.op('pe', lambda e: e.matmul(G.ps[bg][:, 0:T], lhsT=wt[:, kc, 0:128], rhs=xb[:, kc, :],
                                                  start=(kc == 0), stop=(kc == DC - 1)),
                         reads=[wt.name, xb.name], writes=['ps%d' % bg], signal=(kc == DC - 1))
                for kc in range(DC):
                    P.op('pe', lambda e: e.matmul(G.ps[bu][:, 0:T], lhsT=wt[:, kc, 128:256], rhs=xb[:, kc, :],
                                                  start=(kc == 0), stop=(kc == DC - 1)),
                         reads=[wt.name, xb.name], writes=['ps%d' % bu], signal=(kc == DC - 1))
                sgt = sg[j % 2]
                P.op('act', lambda e: e.activation(out=sgt[:, :], in_=G.ps[bg][:, 0:T], func=AF.Silu),
                     reads=['ps%d' % bg], writes=[sgt.name])
                P.op('dve', lambda e: e.tensor_tensor(out=hT[:, j, :], in0=G.ps[bu][:, 0:T], in1=sgt[:, :], op=ALU.mult),
                     reads=['ps%d' % bu, sgt.name], writes=[hT.name])
                if pending:
                    pending.pop(0)()
                elif j >= 8 and j % 2 == 0 and getattr(G, 'extra_jobs', None):
                    G.extra_jobs.pop(0)()
            while pending:
                pending.pop(0)()
            if getattr(G, 'dbg_h', None) is not None:
                P.dma('pool', out=G.dbg_h, in_=hT[:, :, :], reads=[hT.name], writes=['dbg_h'])
            for c in range(DC):
                if c + 1 < DC:
                    load_wd(b, c + 1)
                if b + 1 < NB and c + 2 == DC:
                    load_wgu(b + 1, 0)
                elif b + 1 < NB and c + 2 == DC + 1:
                    load_wgu(b + 1, 1)
                xct = xc[c % 4]
                P.dma('sp', out=xct[:, :], in_=src_f[:, c, t0:t0 + T], reads=['XT_%d' % b], writes=[xct.name])
                wt = wd[(b * DC + c) % 2]
                bo = 4 + (c % 2)
                for fc in range(FC):
                    P.op('pe', lambda e: e.matmul(G.ps[bo][:, 0:T], lhsT=wt[:, fc, :], rhs=hT[:, fc, :],
                                                  start=(fc == 0), stop=(fc == FC - 1)),
                         reads=[wt.name, hT.name], writes=['ps%d' % bo], signal=(fc == FC - 1))
                P.op('dve', lambda e: e.scalar_tensor_tensor(out=z[:, c, :], in0=xct[:, :], scalar=2.0 * ALPHA,
                                                             in1=G.ps[bo][:, 0:T], op0=ALU.mult, op1=ALU.add),
                     reads=[xct.name, 'ps%d' % bo], writes=[zk(z, c)])
                ln_stats_chunk(P, G, z[:, c, :], zk(z, c), zsq[c % 2], zsq[c % 2].name, c, DC, T, 6, 7, acc1, acc2)
            ln_finish_stats(P, G, T, D, 4.0 * LN_EPS, 6, 7, mean_t, rstd_t, tmp_t)
            G.pump_ok = (b + 1 < NB)
            pending = ln_apply_jobs(P, G, z, T, lnidx, mean_t, rstd_t, xbo,
                                    dst_f[:, :, t0:t0 + T], None if dst_b is None else dst_b[:, :, t0:t0 + T],
                                    'XT_%d' % b, 'XB_%d' % b)
        while pending:
            pending.pop(0)()


def tile_w(W):
    K, N = W.shape
    return np.ascontiguousarray(W.reshape(K // 128, 128, N // 128, 128).transpose(2, 1, 0, 3)).reshape(N // 128, 128, (K // 128) * 128)


def tile_wgu(Wg, Wu):
    K, N = Wg.shape
    g = Wg.reshape(K // 128, 128, N // 128, 128).transpose(2, 1, 0, 3)
    u = Wu.reshape(K // 128, 128, N // 128, 128).transpose(2, 1, 0, 3)
    return np.ascontiguousarray(np.concatenate([g, u], axis=3)).reshape(N // 128, 128, (K // 128) * 256)


def build_globals(P, nc, G, st):
    G.ps = [st.enter_context(nc.psum_tensor('psb%d' % i, [128, 512], F32)) for i in range(8)]
    G.ones_f = sb(st, nc, 'ones_f', [128, 128], F32)
    G.ones_b = sb(st, nc, 'ones_b', [128, 128], BF16)
    P.op('dve', lambda e: e.memset(G.ones_f[:, :], 1.0), writes=['ones_f'])
    P.op('dve', lambda e: e.memset(G.ones_b[:, :], 1.0), writes=['ones_b'])
    G.lng = sb(st, nc, 'lng_sb', [128, 8, DC], F32)
    G.lnb = sb(st, nc, 'lnb_sb', [128, 8, DC], F32)
    P.dma('sp', out=G.lng[:, :, :], in_=G.lng_d, writes=['lnp'])
    P.dma('sp', out=G.lnb[:, :, :], in_=G.lnb_d, writes=['lnp2'])


class WT:
    def __init__(self, nc, name, n, F, group):
        self.name, self.n, self.F, self.group = name, n, F, group
        self.src = nc.dram_tensor(name, [n, 128, F], F32, kind='ExternalInput').ap()
        self.dst = nc.dram_tensor(name + '_bf', [n, 128, F], BF16, kind='Internal').ap()
        self.issued = set()

    def jobs(self):
        g = self.group
        out = []
        for i in range(0, self.n, g):
            out.append((self, i))
        return out

    def issue(self, P, i):
        g = self.group
        e = min(self.n, i + g)
        P.dma('poolc', out=self.dst[i:e], in_=self.src[i:e], writes=['%s_%d' % (self.name, i // g)])
        self.issued.add(i // g)

    @property
    def done(self):
        return len(self.issued) == (self.n + self.group - 1) // self.group

    def cast(self, P):
        for w, i in self.jobs():
            if (i // self.group) not in self.issued:
                self.issue(P, i)

    def key(self, j):
        return '%s_%d' % (self.name, j // self.group)

    def tile(self, j):
        return self.dst[j]


MLA_SCALE = (128 + 64) ** -0.5


def evac_copy(P, eng, out_ap, out_key, bank, T, lo=0, np_=128):
    if eng == 'act':
        P.op('act', lambda e: e.activation(out=out_ap, in_=P.G.ps[bank][0:np_, lo:T], func=AF.Copy),
             reads=['ps%d' % bank], writes=[out_key])
    else:
        P.op('dve', lambda e: e.tensor_copy(out=out_ap, in_=P.G.ps[bank][0:np_, lo:T]),
             reads=['ps%d' % bank], writes=[out_key])


def rope_combine(P, G, bankA, bankB, out_ap, out_key, tcol, T, tmpa, tmpb):
    P.op('dve', lambda e: e.tensor_tensor(out=tmpa[0:64, 0:T], in0=G.ps[bankA][0:64, 0:T], in1=G.cos_t[0:64, tcol:tcol + T],
                                          op=ALU.mult), reads=['ps%d' % bankA, 'rope'], writes=[tmpa.name])
    P.op('dve', lambda e: e.tensor_tensor(out=tmpb[0:64, 0:T], in0=G.ps[bankB][0:64, 0:T], in1=G.sin_t[0:64, tcol:tcol + T],
                                          op=ALU.mult), reads=['ps%d' % bankB, 'rope'], writes=[tmpb.name])
    P.op('dve', lambda e: e.tensor_tensor(out=out_ap, in0=tmpa[0:64, 0:T], in1=tmpb[0:64, 0:T], op=ALU.add),
         reads=[tmpa.name, tmpb.name], writes=[out_key])


def mla_phase(P, nc, G, l, seqs):
    T = 512
    NBLK = S // T
    wm = G.win_mla[l]
    with ExitStack() as st:
        cqn = sb(st, nc, 'm_cqn', [128, 4, S], BF16)
        ckvn = sb(st, nc, 'm_ckvn', [128, 4, S], BF16)
        krot = sb(st, nc, 'm_krot', [64, S], BF16)
        xb = [sb(st, nc, 'm_xb%d' % k, [128, DC, T], BF16) for k in range(2)]
        wt = [sb(st, nc, 'm_wt%d' % k, [128, DC, 128], BF16) for k in range(3)]
        craw = sb(st, nc, 'm_craw', [128, 4, T], F32)
        zsq = [sb(st, nc, 'm_zsq%d' % k, [128, T], F32) for k in range(2)]
        rstd = sb(st, nc, 'm_rstd', [128, T], F32)
        tmpa = sb(st, nc, 'm_tmpa', [64, T], F32)
        tmpb = sb(st, nc, 'm_tmpb', [64, T], F32)
        qn = sb(st, nc, 'm_qn', [128, S], BF16)
        qr = sb(st, nc, 'm_qr', [64, S], BF16)
        kn = sb(st, nc, 'm_kn', [128, S], BF16)
        vh = sb(st, nc, 'm_vh', [128, 16, 128], BF16)
        wq = [sb(st, nc, 'm_wq%d' % k, [128, 4, 128], BF16) for k in range(2)]
        wkv = [sb(st, nc, 'm_wkv%d' % k, [128, 4, 128], BF16) for k in range(2)]
        pt = [sb(st, nc, 'm_pt%d' % k, [128, T], BF16) for k in range(4)]
        rec = sb(st, nc, 'm_rec', [128, T], F32)
        ot = [sb(st, nc, 'm_ot%d' % k, [128, T], BF16) for k in range(2)]
        G.cos_t = sb(st, nc, 'm_cos', [64, S], F32)
        G.sin_t = sb(st, nc, 'm_sin', [64, S], F32)

        for s in seqs:
            tb = s * S
            sched = [(b_, oc_) for b_ in range(NBLK) for oc_ in range(9)]

            def issue_w(i_):
                w_ = wt[i_ % 3]
                P.dma('sp', out=w_[:, :, :], in_=wm.tile(sched[i_][1]).rearrange('p (k n) -> p k n', n=128),
                      reads=[wm.key(sched[i_][1])], writes=[w_.name])

            issue_w(0)
            P.dma('sp', out=xb[0][:, :, :], in_=G.XBv[:, :, tb:tb + T], reads=['XB_%d' % (tb // T)], writes=[xb[0].name])
            issue_w(1)
            if s == seqs[0]:
                P.dma('sp', out=G.cos_t[:, :], in_=G.cos_d, writes=['rope'])
                P.dma('sp', out=G.sin_t[:, :], in_=G.sin_d, writes=['rope'], acc=True)
            for b in range(NBLK):
                xbt = xb[b % 2]
                if b + 1 < NBLK:
                    t1 = tb + (b + 1) * T
                    P.dma('sp', out=xb[(b + 1) % 2][:, :, :], in_=G.XBv[:, :, t1:t1 + T], reads=['XB_%d' % (t1 // T)],
                          writes=[xb[(b + 1) % 2].name])
                for oc in range(9):
                    i_cur = b * 9 + oc
                    w = wt[i_cur % 3]
                    if i_cur + 2 < len(sched):
                        issue_w(i_cur + 2)
                    if oc < 8:
                        bank = oc % 2
                        for kc in range(DC):
                            P.op('pe', lambda e: e.matmul(G.ps[bank][:, 0:T], lhsT=w[:, kc, :], rhs=xbt[:, kc, :],
                                                          start=(kc == 0), stop=(kc == DC - 1)),
                                 reads=[w.name, xbt.name], writes=['ps%d' % bank], signal=(kc == DC - 1))
                        c4 = oc % 4
                        evac_copy(P, 'act', craw[:, c4, :], craw.name, bank, T)
                        z_ap = craw[:, c4, :]
                        zq = zsq[oc % 2]
                        P.op('act', lambda e: e.activation(out=zq[:, :], in_=z_ap, func=AF.Square),
                             reads=[craw.name], writes=[zq.name])
                        P.op('pe', lambda e: e.matmul(G.ps[2][:, 0:T], lhsT=G.ones_f[:, :], rhs=zq[:, :],
                                                      start=(c4 == 0), stop=(c4 == 3)),
                             reads=[zq.name, 'ones_f'], writes=['ps2'], signal=True)
                        if c4 == 3:
                            ln_finish_stats(P, G, T, 512, RMS_EPS, None, 2, None, rstd, None, with_mean=False)
                            dstt = cqn if oc < 4 else ckvn
                            nrm = G.qnorm if oc < 4 else G.kvnorm
                            for cc in range(4):
                                P.op('dve', lambda e: e.scalar_tensor_tensor(
                                    out=dstt[:, cc, b * T:(b + 1) * T], in0=craw[:, cc, :], scalar=nrm[:, l, cc:cc + 1],
                                    in1=rstd[:, :], op0=ALU.mult, op1=ALU.mult),
                                    reads=[craw.name, rstd.name, 'smallp'], writes=[dstt.name])
                    else:
                        for half, bank in ((0, 3), (1, 4)):
                            for kc in range(DC):
                                P.op('pe', lambda e: e.matmul(G.ps[bank][0:64, 0:T], lhsT=w[:, kc, half * 64:half * 64 + 64],
                                                              rhs=xbt[:, kc, :], start=(kc == 0), stop=(kc == DC - 1)),
                                     reads=[w.name, xbt.name], writes=['ps%d' % bank], signal=(kc == DC - 1))
                        rope_combine(P, G, 3, 4, krot[0:64, b * T:(b + 1) * T], krot.name, b * T, T, tmpa, tmpb)

            for h in range(8):
                if 1 <= h <= 6 and getattr(G, 'pump', None) is not None:
                    G.pump(1)
                wqa, wqb = wq[0], wq[1]
                wka, wva = wkv[0], wkv[1]
                P.dma('sp', out=wqa[:, :, :], in_=G.wuq[l].tile(2 * h).rearrange('p (k n) -> p k n', n=128),
                      reads=[G.wuq[l].key(2 * h)], writes=[wqa.name])
                P.dma('sp', out=wqb[:, :, :], in_=G.wuq[l].tile(2 * h + 1).rearrange('p (k n) -> p k n', n=128),
                      reads=[G.wuq[l].key(2 * h + 1)], writes=[wqb.name])
                P.dma('sp', out=wka[:, :, :], in_=G.wukv[l].tile(2 * h).rearrange('p (k n) -> p k n', n=128),
                      reads=[G.wukv[l].key(2 * h)], writes=[wka.name])
                P.dma('sp', out=wva[:, :, :], in_=G.wukv[l].tile(2 * h + 1).rearrange('p (k n) -> p k n', n=128),
                      reads=[G.wukv[l].key(2 * h + 1)], writes=[wva.name])
                for b in range(NBLK):
                    cs = slice(b * T, (b + 1) * T)
                    for kc in range(4):
                        P.op('pe', lambda e: e.matmul(G.ps[7][:, 0:T], lhsT=wqa[:, kc, :], rhs=cqn[:, kc, cs],
                                                      start=(kc == 0), stop=(kc == 3)),
                             reads=[wqa.name, cqn.name], writes=['ps7'], signal=(kc == 3))
                    evac_copy(P, 'act', qn[:, cs], qn.name, 7, T)
                    for half, bank in ((0, 5), (1, 6)):
                        for kc in range(4):
                            P.op('pe', lambda e: e.matmul(G.ps[bank][0:64, 0:T], lhsT=wqb[:, kc, half * 64:half * 64 + 64],
                                                          rhs=cqn[:, kc, cs], start=(kc == 0), stop=(kc == 3)),
                                 reads=[wqb.name, cqn.name], writes=['ps%d' % bank], signal=(kc == 3))
                    rope_combine(P, G, 5, 6, qr[0:64, cs], qr.name, b * T, T, tmpa, tmpb)
                    for kc in range(4):
                        P.op('pe', lambda e: e.matmul(G.ps[2][:, 0:T], lhsT=wka[:, kc, :], rhs=ckvn[:, kc, cs],
                                                      start=(kc == 0), stop=(kc == 3)),
                             reads=[wka.name, ckvn.name], writes=['ps2'], signal=(kc == 3))
                    evac_copy(P, 'dve', kn[:, cs], kn.name, 2, T)
                    for tt in range(4):
                        tok = slice(b * T + tt * 128, b * T + (tt + 1) * 128)
                        for kc in range(4):
                            P.op('pe', lambda e: e.matmul(G.ps[1][:, tt * 128:(tt + 1) * 128], lhsT=ckvn[:, kc, tok],
                                                          rhs=wva[:, kc, :], start=(kc == 0), stop=(kc == 3)),
                                 reads=[wva.name, ckvn.name], writes=['ps1'], signal=(tt == 3 and kc == 3))
                    P.op('act', lambda e: e.activation(out=vh[:, 4 * b:4 * b + 4, :].rearrange('p a b -> p (a b)'),
                                                       in_=G.ps[1][:, 0:T], func=AF.Copy),
                         reads=['ps1'], writes=[vh.name])
                tiles = [(qb, kc) for qb in range(NBLK) for kc in range(4 * qb + 4)]

                def emit_scores(idx):
                    qb, kc = tiles[idx]
                    lo = max(0, kc - 4 * qb) * 128
                    bank = idx % 3
                    qs = slice(qb * T + lo, (qb + 1) * T)
                    ks = slice(kc * 128, (kc + 1) * 128)
                    P.op('pe', lambda e: e.matmul(G.ps[bank][:, lo:T], lhsT=kn[:, ks], rhs=qn[:, qs], start=True, stop=False),
                         reads=[kn.name, qn.name], writes=['ps%d' % bank], signal=False)
                    P.op('pe', lambda e: e.matmul(G.ps[bank][:, lo:T], lhsT=krot[0:64, ks], rhs=qr[0:64, qs],
                                                  start=False, stop=True),
                         reads=[krot.name, qr.name], writes=['ps%d' % bank], signal=True)

                emit_scores(0)
                emit_scores(1)
                for idx, (qb, kc) in enumerate(tiles):
                    if idx + 2 < len(tiles):
                        emit_scores(idx + 2)
                    lo = max(0, kc - 4 * qb) * 128
                    bank = idx % 3
                    p_t = pt[idx % 4]
                    P.op('act', lambda e: e.activation(out=p_t[:, lo:T], in_=G.ps[bank][:, lo:T], func=AF.Exp, scale=MLA_SCALE),
                         reads=['ps%d' % bank], writes=[p_t.name])
                    if kc >= 4 * qb:
                        P.op('dve', lambda e: e.tensor_tensor(out=p_t[:, lo:lo + 128], in0=p_t[:, lo:lo + 128],
                                                              in1=G.tri_b[:, :], op=ALU.mult),
                             reads=['tri_b'], writes=[p_t.name])
                    bo, bl = 3 + (qb % 2), 5 + (qb % 2)
                    last = (kc == 4 * qb + 3)
                    P.op('pe', lambda e: e.matmul(G.ps[bo][:, lo:T], lhsT=vh[:, kc, :], rhs=p_t[:, lo:T],
                                                  start=(kc == 0), stop=last),
                         reads=[vh.name, p_t.name], writes=['ps%d' % bo], signal=last)
                    P.op('pe', lambda e: e.matmul(G.ps[bl][:, lo:T], lhsT=G.ones_b[:, :], rhs=p_t[:, lo:T],
                                                  start=(kc == 0), stop=last),
                         reads=['ones_b', p_t.name], writes=['ps%d' % bl], signal=True)
                    if last:
                        o_t = ot[qb % 2]
                        P.op('dve', lambda e: e.reciprocal(out=rec[:, :], in_=G.ps[bl][:, 0:T]),
                             reads=['ps%d' % bl], writes=[rec.name])
                        P.op('dve', lambda e: e.tensor_tensor(out=o_t[:, :], in0=G.ps[bo][:, 0:T], in1=rec[:, :], op=ALU.mult),
                             reads=['ps%d' % bo, rec.name], writes=[o_t.name])
                        t0 = tb + qb * T
                        P.dma('pool', out=G.OA[h * 128:(h + 1) * 128, t0:t0 + T], in_=o_t[:, :],
                              reads=[o_t.name], writes=['OA_%d' % (t0 // T)], acc=True)


def sgu_phase(P, nc, G, l, seqs):
    T = 512
    NBLK = S // T
    wu_w = G.win_u[l]
    wv_w = G.win_v[l]
    with ExitStack() as st:
        xb = [sb(st, nc, 's_xb%d' % k, [128, DC, T], BF16) for k in range(2)]
        wv = sb(st, nc, 's_wv', [128, DC, 512], BF16)
        wu = [sb(st, nc, 's_wu%d' % k, [128, DC, 128], BF16) for k in range(4)]
        u_t = sb(st, nc, 's_u', [128, 4, T], F32)
        vt = [sb(st, nc, 's_vt%d' % k, [128, T], F32) for k in range(2)]
        vn = [sb(st, nc, 's_vn%d' % k, [128, T], BF16) for k in range(4)]
        stats = sb(st, nc, 's_stats', [128, 6], F32)
        mv = sb(st, nc, 's_mv', [128, 2], F32)
        rs = sb(st, nc, 's_rs', [128, 1], F32)
        tmp = sb(st, nc, 's_tmp', [128, T], F32)
        ob = sb(st, nc, 's_ob', [128, 4, T], BF16)
        sgu_g = sb(st, nc, 's_g', [128, 512], F32)
        sgu_bb = sb(st, nc, 's_b', [128, 512], F32)
        wsf = sb(st, nc, 's_wsf', [128, 4, 128], F32)
        wsT = sb(st, nc, 's_wsT', [128, 4, 128], BF16)
        bs_bc = sb(st, nc, 's_bs', [128, 4, 4, 128], F32)
        P.dma('sp', out=sgu_g[:, :], in_=G.sgu_g_d[l:l + 1, :].to_broadcast([128, 512]), writes=['smallp_s'])
        P.dma('sp', out=sgu_bb[:, :], in_=G.sgu_b_d[l:l + 1, :].to_broadcast([128, 512]), writes=['smallp_s'], acc=True)
        P.dma('sp', out=wsf[:, :, :], in_=G.wsT_d[l], writes=[wsf.name])
        for rep in range(4):
            P.dma('sp', out=bs_bc[:, :, rep, :], in_=G.bs_d[l:l + 1].to_broadcast([128, 4, 128]),
                  writes=['smallp_s'], acc=True)
        for g in range(4):
            P.op('dve', lambda e: e.tensor_tensor(out=wsT[:, g, :], in0=wsf[:, g, :], in1=G.tri_f[:, :], op=ALU.mult),
                 reads=[wsf.name, 'tri_f'], writes=['wsT'])

        for g in range(4):
            P.dma('sp', out=wu[g][:, :, :], in_=wu_w.tile(g).rearrange('p (k n) -> p k n', n=128),
                  reads=[wu_w.key(g)], writes=[wu[g].name])
        P.dma('sp', out=wv[:, :, :], in_=wv_w.tile(0).rearrange('p (k n) -> p k n', n=512),
              reads=[wv_w.key(0)], writes=[wv.name])
        for s in seqs:
            tb = s * S
            P.dma('sp', out=xb[0][:, :, :], in_=G.XBv[:, :, tb:tb + T], reads=['XB_%d' % (tb // T)], writes=[xb[0].name])
            for b in range(NBLK):
                xbt = xb[b % 2]
                if b + 1 < NBLK:
                    t1 = tb + (b + 1) * T
                    P.dma('sp', out=xb[(b + 1) % 2][:, :, :], in_=G.XBv[:, :, t1:t1 + T], reads=['XB_%d' % (t1 // T)],
                          writes=[xb[(b + 1) % 2].name])
                for g in range(4):
                    bank = g % 2
                    for kc in range(DC):
                        P.op('pe', lambda e: e.matmul(G.ps[bank][:, 0:T], lhsT=wu[g][:, kc, :], rhs=xbt[:, kc, :],
                                                      start=(kc == 0), stop=(kc == DC - 1)),
                             reads=[wu[g].name, xbt.name], writes=['ps%d' % bank], signal=(kc == DC - 1))
                    P.op('act', lambda e: e.activation(out=u_t[:, g, :], in_=G.ps[bank][:, 0:T], func=AF.Gelu),
                         reads=['ps%d' % bank], writes=[u_t.name])
                for tt in range(4):
                    bank = 2 + (tt % 2)
                    for kc in range(DC):
                        P.op('pe', lambda e: e.matmul(G.ps[bank][:, 0:T], lhsT=xbt[:, kc, tt * 128:(tt + 1) * 128],
                                                      rhs=wv[:, kc, :], start=(kc == 0), stop=(kc == DC - 1)),
                             reads=[wv.name, xbt.name], writes=['ps%d' % bank], signal=(kc == DC - 1))
                    v_t = vt[tt % 2]
                    P.op('act', lambda e: e.activation(out=v_t[:, :], in_=G.ps[bank][:, 0:T], func=AF.Gelu),
                         reads=['ps%d' % bank], writes=[v_t.name])
                    P.op('dve', lambda e: e.bn_stats(out=stats[:, :], in_=v_t[:, :]), reads=[v_t.name], writes=[stats.name])
                    P.op('dve', lambda e: e.bn_aggr(out=mv[:, :], in_=stats[:, :]), reads=[stats.name], writes=[mv.name],
                         sync_same=[stats.name])
                    P.op('dve', lambda e: e.tensor_scalar(out=rs[:, :], in0=mv[:, 1:2], scalar1=LN_EPS, scalar2=None, op0=ALU.add),
                         reads=[mv.name], writes=[rs.name], sync_same=[mv.name])
                    P.op('act', lambda e: e.activation(out=rs[:, :], in_=rs[:, :], func=AF.Sqrt), reads=[rs.name], writes=[rs.name])
                    P.op('dve', lambda e: e.reciprocal(out=rs[:, :], in_=rs[:, :]), reads=[rs.name], writes=[rs.name])
                    P.op('dve', lambda e: e.tensor_scalar(out=v_t[:, :], in0=v_t[:, :], scalar1=mv[:, 0:1], scalar2=rs[:, 0:1],
                                                          op0=ALU.subtract, op1=ALU.mult),
                         reads=[mv.name, rs.name], writes=[v_t.name], sync_same=[mv.name, rs.name])
                    P.op('dve', lambda e: e.tensor_tensor(out=v_t[:, :], in0=v_t[:, :], in1=sgu_g[:, :], op=ALU.mult),
                         reads=['smallp_s'], writes=[v_t.name])
                    P.op('dve', lambda e: e.tensor_tensor(out=vn[tt][:, :], in0=v_t[:, :], in1=sgu_bb[:, :], op=ALU.add),
                         reads=['smallp_s', v_t.name], writes=[vn[tt].name])
                for g in range(4):
                    bank = 4 + (g % 2)
                    for tt in range(4):
                        P.op('pe', lambda e: e.matmul(G.ps[bank][:, tt * 128:(tt + 1) * 128], lhsT=vn[tt][:, g * 128:(g + 1) * 128],
                                                      rhs=wsT[:, g, :], start=True, stop=True),
                             reads=[vn[tt].name, 'wsT'], writes=['ps%d' % bank], signal=(tt == 3))
                    P.op('dve', lambda e: e.tensor_tensor(out=tmp[:, :], in0=G.ps[bank][:, 0:T],
                                                          in1=bs_bc[:, g, :, :].rearrange('p a b -> p (a b)'), op=ALU.add),
                         reads=['ps%d' % bank, 'smallp_s'], writes=[tmp.name])
                    P.op('dve', lambda e: e.tensor_tensor(out=ob[:, g, :], in0=tmp[:, :], in1=u_t[:, g, :], op=ALU.mult),
                         reads=[tmp.name, u_t.name], writes=[ob.name])
                t0 = tb + b * T
                P.dma('pool', out=G.OBv[:, :, t0:t0 + T], in_=ob[:, :, :], reads=[ob.name], writes=['OB_%d' % (t0 // T)])


DIL = ((1, 16), (4, 4), (16, 1))
DIL_SCALE = 128 ** -0.5


def dil_phase(P, nc, G, l, seqs):
    T = 512
    NBLK = S // T
    wq_w = G.win_qkv[l]
    with ExitStack() as st:
        xb = [sb(st, nc, 'd_xb%d' % k, [128, DC, T], BF16) for k in range(2)]
        wt = [sb(st, nc, 'd_wt%d' % k, [128, DC, 128], BF16) for k in range(3)]
        qkv2 = [sb(st, nc, 'd_qkv%d' % k, [128, 9, S], BF16) for k in range(2)]
        vtm2 = [sb(st, nc, 'd_vtm%d' % k, [128, 3, 16, 128], BF16) for k in range(2)]
        nd = sb(st, nc, 'd_nd', [128, 2, S], F32)
        e_t = [sb(st, nc, 'd_e%d' % k, [128, 256], F32) for k in range(4)]
        pt = [sb(st, nc, 'd_pt%d' % k, [128, 256], BF16) for k in range(4)]
        rec = sb(st, nc, 'd_rec', [128, T], F32)
        oc_t = [sb(st, nc, 'd_oc%d' % k, [128, T], BF16) for k in range(2)]
        psT = G.ps[2][:, :].bitcast(BF16)
        G.E = sb(st, nc, 'd_E', [128, 12, 256], F32)
        items = [(s_, hh_) for s_ in seqs for hh_ in range(4)]
        wcount = [0]
        xcount = [0]

        def stage_a(k):
            s_, hh = items[k]
            tb = s_ * S
            qkv, vtm = qkv2[k % 2], vtm2[k % 2]
            jobs = []
            sched = [(b_, oc_) for b_ in range(NBLK) for oc_ in range(9)]
            base = wcount[0]
            wcount[0] += len(sched)
            xbase = xcount[0]
            xcount[0] += NBLK

            def issue_w(i_):
                w_ = wt[(base + i_) % 3]
                ti = hh * 9 + sched[i_][1]
                P.dma('sp', out=w_[:, :, :], in_=wq_w.tile(ti).rearrange('p (k n) -> p k n', n=128),
                      reads=[wq_w.key(ti)], writes=[w_.name])

            def load_xb(b):
                t1 = tb + b * T
                x_ = xb[(xbase + b) % 2]
                P.dma('sp', out=x_[:, :, :], in_=G.XBv[:, :, t1:t1 + T], reads=['XB_%d' % (t1 // T)], writes=[x_.name])

            def group(i_cur):
                b, oc = sched[i_cur]
                if i_cur == 0:
                    issue_w(0)
                    load_xb(0)
                    issue_w(1)
                    if k == 0:
                        P.dma('sp', out=G.E[:, :, :], in_=G.E_d, reads=['E_d'], writes=['E'])
                if oc == 0 and b + 1 < NBLK:
                    load_xb(b + 1)
                if i_cur + 2 < len(sched):
                    issue_w(i_cur + 2)
                w = wt[(base + i_cur) % 3]
                xbt = xb[(xbase + b) % 2]
                bank = i_cur % 2
                for kc in range(DC):
                    P.op('pe', lambda e: e.matmul(G.ps[bank][:, 0:T], lhsT=w[:, kc, :], rhs=xbt[:, kc, :],
                                                  start=(kc == 0), stop=(kc == DC - 1)),
                         reads=[w.name, xbt.name], writes=['ps%d' % bank], signal=(kc == DC - 1))
                evac_copy(P, 'act' if i_cur % 2 == 0 else 'dve', qkv[:, oc, b * T:(b + 1) * T], qkv.name, bank, T)
                if oc == 8 and b == NBLK - 1 and hh <= 2 and getattr(G, 'pump', None) is not None:
                    G.pump(1)

            def transp(g, q):
                dil, nb = DIL[g]
                for q4 in range(4):
                    bi = q * 4 + q4
                    r, kb = bi // nb, bi % nb
                    start = r + dil * 128 * kb
                    ks = slice(start, start + dil * 127 + 1, dil)
                    P.op('pe', lambda e: e.transpose(out=psT[:, q4 * 128:(q4 + 1) * 128], in_=qkv[:, 6 + g, ks],
                                                     identity=G.ident_b[:, :]),
                         reads=[qkv.name, 'ident_b'], writes=['ps2'], signal=(q4 == 3))
                P.op('dve', lambda e: e.tensor_copy(
                    out=vtm[:, g, q * 4:q * 4 + 4, :].rearrange('p a b -> p (a b)'), in_=psT[:, 0:512]),
                    reads=['ps2'], writes=[vtm.name])

            for i_ in range(len(sched)):
                jobs.append(lambda i_=i_: group(i_))
            for g in range(3):
                for q in range(4):
                    jobs.append(lambda g=g, q=q: transp(g, q))
            return jobs

        def stage_b(k):
            s_, hh = items[k]
            tb = s_ * S
            qkv, vtm = qkv2[k % 2], vtm2[k % 2]
            blocks = [(g, bi) for g in range(3) for bi in range(16)]
            nblk = len(blocks)

            def geom(g, bi):
                dil, nb = DIL[g]
                r, kb = bi // nb, bi % nb
                start = r + dil * 128 * kb
                nq = 128 if kb == nb - 1 else 256
                return slice(start, start + dil * 127 + 1, dil), slice(start, start + dil * (nq - 1) + 1, dil), nq

            def emit_scores(idx):
                g, bi = blocks[idx]
                ks, qs, nq = geom(g, bi)
                bank = 3 + idx % 3
                P.op('pe', lambda e: e.matmul(G.ps[bank][:, 0:nq], lhsT=qkv[:, 3 + g, ks], rhs=qkv[:, g, qs],
                                              start=True, stop=True),
                     reads=[qkv.name], writes=['ps%d' % bank], signal=True)

            def emit_exp(idx):
                g, bi = blocks[idx]
                ks, qs, nq = geom(g, bi)
                bank = 3 + idx % 3
                et, p_t = e_t[idx % 4], pt[idx % 4]
                P.op('act', lambda e: e.activation(out=et[:, 0:nq], in_=G.ps[bank][:, 0:nq], func=AF.Exp, scale=DIL_SCALE),
                     reads=['ps%d' % bank], writes=[et.name])
                P.op('dve', lambda e: e.tensor_tensor(out=p_t[:, 0:nq], in0=et[:, 0:nq], in1=G.E[:, 4 * g + hh, 0:nq],
                                                      op=ALU.mult),
                     reads=[et.name, 'E'], writes=[p_t.name])

            def emit_pv(idx):
                g, bi = blocks[idx]
                ks, qs, nq = geom(g, bi)
                p_t = pt[idx % 4]
                bo = 6 + (idx % 2)
                P.op('pe', lambda e: e.matmul(G.ps[bo][:, 0:nq], lhsT=vtm[:, g, bi, :], rhs=p_t[:, 0:nq],
                                              start=True, stop=True),
                     reads=[vtm.name, p_t.name], writes=['ps%d' % bo], signal=False)
                P.op('pe', lambda e: e.matmul(G.ps[bo][:, 256:256 + nq], lhsT=G.ones_b[:, :], rhs=p_t[:, 0:nq],
                                              start=True, stop=True),
                     reads=['ones_b', p_t.name], writes=['ps%d' % bo], signal=True)
                P.op('dve', lambda e: e.tensor_tensor(
                    out=nd[:, :, qs], in0=nd[:, :, qs],
                    in1=G.ps[bo][:, :].rearrange('p (a b) -> p a b', a=2)[:, :, 0:nq], op=ALU.add),
                    reads=['ps%d' % bo], writes=[nd.name])

            def step(t):
                if t == -1:
                    P.op('dve', lambda e: e.memset(nd[:, :, :], 0.0), writes=[nd.name])
                    emit_scores(0)
                    emit_scores(1)
                    emit_exp(0)
                    return
                if t + 2 < nblk:
                    emit_scores(t + 2)
                if t + 1 < nblk:
                    emit_exp(t + 1)
                emit_pv(t)

            def norm(b):
                cs = slice(b * T, (b + 1) * T)
                o_t = oc_t[b % 2]
                P.op('dve', lambda e: e.reciprocal(out=rec[:, :], in_=nd[:, 1, cs]), reads=[nd.name], writes=[rec.name])
                P.op('dve', lambda e: e.tensor_tensor(out=o_t[:, :], in0=nd[:, 0, cs], in1=rec[:, :], op=ALU.mult),
                     reads=[nd.name, rec.name], writes=[o_t.name])
                t0 = tb + b * T
                P.dma('pool', out=G.OC[hh * 128:(hh + 1) * 128, t0:t0 + T], in_=o_t[:, :],
                      reads=[o_t.name], writes=['OC_%d' % (t0 // T)], acc=True)

            jobs = [lambda t=t: step(t) for t in range(-1, nblk)]
            jobs += [lambda b=b: norm(b) for b in range(NBLK)]
            return jobs

        prevB = []
        for k in range(len(items)):
            A = stage_a(k)
            na, nb_ = len(A), len(prevB)
            done_b = 0
            for i_, job in enumerate(A):
                job()
                want = (nb_ * (i_ + 1)) // na
                while done_b < want:
                    prevB.pop(0)()
                    done_b += 1
            while prevB:
                prevB.pop(0)()
            prevB = stage_b(k)
        while prevB:
            prevB.pop(0)()


def merge_phase(P, nc, G, l, NT, lnidx):
    T = 512
    NB = NT // T
    wg_w, wb_w, wo_w = G.win_gate[l], G.wbr[l], G.wout[l]
    with ExitStack() as st:
        xb = sb(st, nc, 'g_xb', [128, DC, T], BF16)
        br = sb(st, nc, 'g_br', [128, DC, T], BF16)
        mg = sb(st, nc, 'g_mg', [128, DC, T], BF16)
        z = sb(st, nc, 'g_z', [128, DC, T], F32)
        xbo = sb(st, nc, 'g_xbo', [128, DC, T], BF16)
        wg = [sb(st, nc, 'g_wg%d' % k, [128, DC, 128], BF16) for k in range(6)]
        wb = [sb(st, nc, 'g_wb%d' % k, [128, DC, 128], BF16) for k in range(2)]
        wo = [sb(st, nc, 'g_wo%d' % k, [128, DC, 128], BF16) for k in range(8)]
        sig = [sb(st, nc, 'g_sig%d' % k, [128, T], F32) for k in range(3)]
        ta = sb(st, nc, 'g_ta', [128, T], F32)
        tb_ = sb(st, nc, 'g_tb', [128, T], F32)
        xc = [sb(st, nc, 'g_xc%d' % k, [128, T], F32) for k in range(4)]
        zsq = [sb(st, nc, 'g_zsq%d' % k, [128, T], F32) for k in range(2)]
        mean_t = sb(st, nc, 'g_mean', [128, T], F32)
        rstd_t = sb(st, nc, 'g_rstd', [128, T], F32)
        tmp_t = sb(st, nc, 'g_tmp', [128, T], F32)
        acc1 = sb(st, nc, 'g_acc1', [128, T], F32)
        acc2 = sb(st, nc, 'g_acc2', [128, T], F32)

        def load_gate(b, c):
            for bi in range(3):
                w_ = wg[((b * DC + c) % 2) * 3 + bi]
                ti = bi * DC + c
                P.dma('sp', out=w_[:, :, :], in_=wg_w.tile(ti).rearrange('p (k n) -> p k n', n=128),
                      reads=[wg_w.key(ti)], writes=[w_.name])
            w_ = wb[(b * DC + c) % 2]
            P.dma('sp', out=w_[:, :, :], in_=wb_w.tile(c).rearrange('p (k n) -> p k n', n=128),
                  reads=[wb_w.key(c)], writes=[w_.name])

        def load_x(b, c):
            t0_ = b * T
            P.dma('sp', out=xc[c % 4][:, :], in_=G.XTv[:, c, t0_:t0_ + T], reads=['XT_%d' % b], writes=[xc[c % 4].name])

        def load_wo(b, c):
            w_ = wo[(b * DC + c) % 8]
            P.dma('sp', out=w_[:, :, :], in_=wo_w.tile(c).rearrange('p (k n) -> p k n', n=128),
                  reads=[wo_w.key(c)], writes=[w_.name])

        pending = []
        for b in range(NB):
            t0 = b * T
            P.dma('sp', out=xb[:, :, :], in_=G.XBv[:, :, t0:t0 + T], reads=['XB_%d' % b], writes=[xb.name])
            P.dma('sp', out=br[:, 0:8, :], in_=G.OAv[:, :, t0:t0 + T], reads=['OA_%d' % b], writes=[br.name])
            P.dma('sp', out=br[:, 8:12, :], in_=G.OBv[:, :, t0:t0 + T], reads=['OB_%d' % b], writes=[br.name], acc=True)
            P.dma('sp', out=br[:, 12:16, :], in_=G.OCv[:, :, t0:t0 + T], reads=['OC_%d' % b], writes=[br.name], acc=True)
            if b == 0:
                load_gate(b, 0)
            for c in range(DC):
                if c + 1 < DC:
                    load_gate(b, c + 1)
                if 4 <= c < 12:
                    load_wo(b, c - 4)
                if c >= 12:
                    load_x(b, c - 12)
                par = (b * DC + c) % 2
                for bi in range(3):
                    w_ = wg[par * 3 + bi]
                    for kc in range(DC):
                        P.op('pe', lambda e: e.matmul(G.ps[bi][:, 0:T], lhsT=w_[:, kc, :], rhs=xb[:, kc, :],
                                                      start=(kc == 0), stop=(kc == DC - 1)),
                             reads=[w_.name, xb.name], writes=['ps%d' % bi], signal=(kc == DC - 1))
                w_ = wb[par]
                for bi, (k0, k1) in enumerate(((0, 8), (8, 12), (12, 16))):
                    for kc in range(k0, k1):
                        P.op('pe', lambda e: e.matmul(G.ps[3 + bi][:, 0:T], lhsT=w_[:, kc, :], rhs=br[:, kc, :],
                                                      start=(kc == k0), stop=(kc == k1 - 1)),
                             reads=[w_.name, br.name], writes=['ps%d' % (3 + bi)], signal=(kc == k1 - 1))
                for bi in range(3):
                    P.op('act', lambda e: e.activation(out=sig[bi][:, :], in_=G.ps[bi][:, 0:T], func=AF.Sigmoid),
                         reads=['ps%d' % bi], writes=[sig[bi].name])
                P.op('dve', lambda e: e.tensor_tensor(out=ta[:, :], in0=G.ps[3][:, 0:T], in1=sig[0][:, :], op=ALU.mult),
                     reads=['ps3', sig[0].name], writes=[ta.name])
                P.op('dve', lambda e: e.tensor_tensor(out=tb_[:, :], in0=G.ps[4][:, 0:T], in1=sig[1][:, :], op=ALU.mult),
                     reads=['ps4', sig[1].name], writes=[tb_.name])
                P.op('dve', lambda e: e.tensor_tensor(out=ta[:, :], in0=ta[:, :], in1=tb_[:, :], op=ALU.add),
                     reads=[tb_.name], writes=[ta.name])
                P.op('dve', lambda e: e.tensor_tensor(out=tb_[:, :], in0=G.ps[5][:, 0:T], in1=sig[2][:, :], op=ALU.mult),
                     reads=['ps5', sig[2].name], writes=[tb_.name])
                P.op('dve', lambda e: e.tensor_tensor(out=mg[:, c, :], in0=ta[:, :], in1=tb_[:, :], op=ALU.add),
                     reads=[ta.name, tb_.name], writes=[mg.name])
                if pending:
                    pending.pop(0)()
            while pending:
                pending.pop(0)()
            for c in range(DC):
                if c + 2 == DC and b + 1 < NB:
                    load_gate(b + 1, 0)
                xct = xc[c % 4]
                w_ = wo[(b * DC + c) % 8]
                bo = c % 2
                for kc in range(DC):
                    P.op('pe', lambda e: e.matmul(G.ps[bo][:, 0:T], lhsT=w_[:, kc, :], rhs=mg[:, kc, :],
                                                  start=(kc == 0), stop=(kc == DC - 1)),
                         reads=[w_.name, mg.name], writes=['ps%d' % bo], signal=(kc == DC - 1))
                if c + 8 < DC:
                    load_wo(b, c + 8)
                P.op('dve', lambda e: e.scalar_tensor_tensor(out=z[:, c, :], in0=xct[:, :], scalar=ALPHA,
                                                             in1=G.ps[bo][:, 0:T], op0=ALU.mult, op1=ALU.add),
                     reads=[xct.name, 'ps%d' % bo], writes=[zk(z, c)])
                if c + 4 < DC:
                    load_x(b, c + 4)
                ln_stats_chunk(P, G, z[:, c, :], zk(z, c), zsq[c % 2], zsq[c % 2].name, c, DC, T, 6, 7, acc1, acc2)
            ln_finish_stats(P, G, T, D, LN_EPS, 6, 7, mean_t, rstd_t, tmp_t)
            G.pump_ok = (b + 1 < NB)
            pending = ln_apply_jobs(P, G, z, T, lnidx, mean_t, rstd_t, xbo,
                                    G.XTv[:, :, t0:t0 + T], G.XBv[:, :, t0:t0 + T], 'XT_%d' % b, 'XB_%d' % b)
        while pending:
            pending.pop(0)()


XA_SCALE = 128 ** -0.5


def xattn_phase(P, nc, G, l, seqs, lnidx):
    T = 512
    NBLK = S // T
    wq_w, wkv_w, wo_w = G.xa_wq[l], G.xa_wkv[l], G.xa_wo[l]
    with ExitStack() as st:
        memT = sb(st, nc, 'x_memT', [128, DC, MEM], BF16)
        wt = [sb(st, nc, 'x_wt%d' % k, [128, DC, 128], BF16) for k in range(3)]
        wq = [sb(st, nc, 'x_wq%d' % k, [128, DC, 128], BF16) for k in range(4)]
        wo = [sb(st, nc, 'x_wo%d' % k, [128, 4, 128], BF16) for k in range(16)]
        kx = sb(st, nc, 'x_kx', [128, 4, MEM], BF16)
        vx = sb(st, nc, 'x_vx', [128, 4, 2, 128], BF16)
        xb = [sb(st, nc, 'x_xb%d' % k, [128, DC, T], BF16) for k in range(2)]
        qx = sb(st, nc, 'x_qx', [128, 4, T], BF16)
        ox = sb(st, nc, 'x_ox', [128, 4, T], BF16)
        pt = [sb(st, nc, 'x_pt%d' % k, [128, T], BF16) for k in range(2)]
        rec = sb(st, nc, 'x_rec', [128, T], F32)
        z = sb(st, nc, 'x_z', [128, DC, T], F32)
        xbo = sb(st, nc, 'x_xbo', [128, DC, T], BF16)
        xc = [sb(st, nc, 'x_xc%d' % k, [128, T], F32) for k in range(4)]
        zsq = [sb(st, nc, 'x_zsq%d' % k, [128, T], F32) for k in range(4)]
        mean_t = sb(st, nc, 'x_mean', [128, T], F32)
        rstd_t = sb(st, nc, 'x_rstd', [128, T], F32)
        tmp_t = sb(st, nc, 'x_tmp', [128, T], F32)
        acc1 = sb(st, nc, 'x_acc1', [128, T], F32)
        acc2 = sb(st, nc, 'x_acc2', [128, T], F32)

        for h in range(4):
            P.dma('sp', out=wq[h][:, :, :], in_=wq_w.tile(h).rearrange('p (k n) -> p k n', n=128),
                  reads=[wq_w.key(h)], writes=[wq[h].name])
        for c in range(16):
            P.dma('sp', out=wo[c][:, :, :], in_=wo_w.tile(c).rearrange('p (k n) -> p k n', n=128),
                  reads=[wo_w.key(c)], writes=[wo[c].name])
        for s in seqs:
            tb = s * S
            P.dma('sp', out=memT[:, :, :], in_=G.MEMBv[:, :, s * MEM:(s + 1) * MEM], reads=['MEMB'], writes=[memT.name])
            for i in range(8):
                w_ = wt[i % 3]
                P.dma('sp', out=w_[:, :, :], in_=wkv_w.tile(i).rearrange('p (k n) -> p k n', n=128),
                      reads=[wkv_w.key(i)], writes=[w_.name])
                if i < 4:
                    for kc in range(DC):
                        P.op('pe', lambda e: e.matmul(G.ps[0][:, 0:MEM], lhsT=w_[:, kc, :], rhs=memT[:, kc, :],
                                                      start=(kc == 0), stop=(kc == DC - 1)),
                             reads=[w_.name, memT.name], writes=['ps0'], signal=(kc == DC - 1))
                    evac_copy(P, 'act', kx[:, i, :], kx.name, 0, MEM)
                else:
                    h = i - 4
                    for mt in range(2):
                        for kc in range(DC):
                            P.op('pe', lambda e: e.matmul(G.ps[1][:, mt * 128:(mt + 1) * 128],
                                                          lhsT=memT[:, kc, mt * 128:(mt + 1) * 128], rhs=w_[:, kc, :],
                                                          start=(kc == 0), stop=(kc == DC - 1)),
                                 reads=[w_.name, memT.name], writes=['ps1'], signal=(mt == 1 and kc == DC - 1))
                    evac_copy(P, 'act', vx[:, h, :, :].rearrange('p a b -> p (a b)'), vx.name, 1, 256)
            P.dma('sp', out=xb[0][:, :, :], in_=G.XBv[:, :, tb:tb + T], reads=['XB_%d' % (tb // T)], writes=[xb[0].name])
            pending = []
            for b in range(NBLK):
                xbt = xb[b % 2]
                t0 = tb + b * T
                gb = t0 // T
                if b + 1 < NBLK:
                    t1 = t0 + T
                    P.dma('sp', out=xb[(b + 1) % 2][:, :, :], in_=G.XBv[:, :, t1:t1 + T], reads=['XB_%d' % (t1 // T)],
                          writes=[xb[(b + 1) % 2].name])
                for h in range(4):
                    bank = h % 2
                    for kc in range(DC):
                        P.op('pe', lambda e: e.matmul(G.ps[bank][:, 0:T], lhsT=wq[h][:, kc, :], rhs=xbt[:, kc, :],
                                                      start=(kc == 0), stop=(kc == DC - 1)),
                             reads=[wq[h].name, xbt.name], writes=['ps%d' % bank], signal=(kc == DC - 1))
                    evac_copy(P, 'act', qx[:, h, :], qx.name, bank, T)
                    for _ in range(2):
                        if pending:
                            pending.pop(0)()
                for h in range(4):
                    for _ in range(3):
                        if pending:
                            pending.pop(0)()
                    bo, bl = 4 + (h % 2), 6 + (h % 2)
                    for mt in range(2):
                        bank = 2 + mt
                        P.op('pe', lambda e: e.matmul(G.ps[bank][:, 0:T], lhsT=kx[:, h, mt * 128:(mt + 1) * 128], rhs=qx[:, h, :],
                                                      start=True, stop=True),
                             reads=[kx.name, qx.name], writes=['ps%d' % bank], signal=True)
                    for mt in range(2):
                        bank = 2 + mt
                        p_t = pt[mt]
                        P.op('act', lambda e: e.activation(out=p_t[:, :], in_=G.ps[bank][:, 0:T], func=AF.Exp, scale=XA_SCALE),
                             reads=['ps%d' % bank], writes=[p_t.name])
                        P.op('pe', lambda e: e.matmul(G.ps[bo][:, 0:T], lhsT=vx[:, h, mt, :], rhs=p_t[:, :],
                                                      start=(mt == 0), stop=(mt == 1)),
                             reads=[vx.name, p_t.name], writes=['ps%d' % bo], signal=(mt == 1))
                        P.op('pe', lambda e: e.matmul(G.ps[bl][:, 0:T], lhsT=G.ones_b[:, :], rhs=p_t[:, :],
                                                      start=(mt == 0), stop=(mt == 1)),
                             reads=['ones_b', p_t.name], writes=['ps%d' % bl], signal=True)
                    P.op('dve', lambda e: e.reciprocal(out=rec[:, :], in_=G.ps[bl][:, 0:T]), reads=['ps%d' % bl], writes=[rec.name])
                    P.op('dve', lambda e: e.tensor_tensor(out=ox[:, h, :], in0=G.ps[bo][:, 0:T], in1=rec[:, :], op=ALU.mult),
                         reads=['ps%d' % bo, rec.name], writes=[ox.name])
                while pending:
                    pending.pop(0)()
                for c in range(DC):
                    xct = xc[c % 4]
                    P.dma('sp', out=xct[:, :], in_=G.XTv[:, c, t0:t0 + T], reads=['XT_%d' % gb], writes=[xct.name])
                    bo = c % 2
                    for kc in range(4):
                        P.op('pe', lambda e: e.matmul(G.ps[bo][:, 0:T], lhsT=wo[c][:, kc, :], rhs=ox[:, kc, :],
                                                      start=(kc == 0), stop=(kc == 3)),
                             reads=[wo[c].name, ox.name], writes=['ps%d' % bo], signal=(kc == 3))
                    P.op('dve', lambda e: e.scalar_tensor_tensor(out=z[:, c, :], in0=xct[:, :], scalar=ALPHA,
                                                                 in1=G.ps[bo][:, 0:T], op0=ALU.mult, op1=ALU.add),
                         reads=[xct.name, 'ps%d' % bo], writes=[zk(z, c)])
                    ln_stats_chunk(P, G, z[:, c, :], zk(z, c), zsq[c % 4], zsq[c % 4].name, c, DC, T, 6, 7, acc1, acc2, mode='dve')
                ln_finish_stats(P, G, T, D, LN_EPS, 6, 7, mean_t, rstd_t, tmp_t)
                G.pump_ok = (b + 1 < NBLK)
                pending = ln_apply_jobs(P, G, z, T, lnidx, mean_t, rstd_t, xbo,
                                        G.XTv[:, :, t0:t0 + T], G.XBv[:, :, t0:t0 + T], 'XT_%d' % gb, 'XB_%d' % gb)
            while pending:
                pending.pop(0)()


WSPECS = [
    ('win_mla', 9, 2048, 9), ('win_u', 4, 2048, 4), ('win_v', 1, 8192, 1), ('win_qkv', 36, 2048, 9),
    ('win_gate', 48, 2048, 8), ('wuq', 16, 512, 16), ('wukv', 16, 512, 16), ('wbr', 16, 2048, 8),
    ('wout', 16, 2048, 8), ('xa_wq', 4, 2048, 4), ('xa_wkv', 8, 2048, 8), ('xa_wo', 16, 512, 16),
]


def e_table_jobs(P, nc, G, st, bank=5):
    rel = sb(st, nc, 'e_rel', [33, 12], F32)
    oh = sb(st, nc, 'e_oh', [33, 3, 385], F32)
    tmp = [sb(st, nc, 'e_tmp%d' % k, [33, 385], F32) for k in range(2)]
    trow = [sb(st, nc, 'e_trow%d' % k, [128, 385], F32) for k in range(2)]

    def head(h):
        if h == 0:
            P.dma('sp', out=rel[:, :], in_=G.rel_d, writes=[rel.name])
            P.dma('sp', out=oh[:, :, :], in_=G.oh_d, writes=[oh.name])
        g = h // 4
        t_, r_ = tmp[h % 2], trow[h % 2]
        P.op('dve', lambda e: e.tensor_scalar(out=t_[:, :], in0=oh[:, g, :], scalar1=rel[:, h:h + 1], scalar2=None,
                                              op0=ALU.mult),
             reads=[rel.name, oh.name], writes=[t_.name])
        P.op('pe', lambda e: e.matmul(G.ps[bank][:, 0:385], lhsT=G.ones_f[0:33, :], rhs=t_[:, :], start=True, stop=True),
             reads=[t_.name, 'ones_f'], writes=['ps%d' % bank])
        P.op('act', lambda e: e.activation(out=r_[:, :], in_=G.ps[bank][:, 0:385], func=AF.Exp),
             reads=['ps%d' % bank], writes=[r_.name])
        P.dma('pool', out=G.gm_d[h].rearrange('(k c) -> k c', c=385), in_=r_[:, :], reads=[r_.name], writes=['gm%d' % h])
        P.dma('pool', out=G.E_d[:, h, :],
              in_=G.gm_d[h][127:127 + 128 * 384].rearrange('(k c) -> k c', c=384)[:, 0:256],
              reads=['gm%d' % h], writes=['E_d'], acc=(h > 0))

    return [lambda h=h: head(h) for h in range(12)]


def build_program(NSEQ=2, NLAYERS=2, upto=None, dbg=False):
    NT = NSEQ * S
    nc = bass.Bass("TRN2", target_bir_lowering=False)
    G = Ctx()
    G.NT = NT

    def din(name, shape, dt=F32):
        return nc.dram_tensor(name, list(shape), dt, kind='ExternalInput').ap()

    def dint(name, shape, dt):
        return nc.dram_tensor(name, list(shape), dt, kind='Internal').ap()

    xT = din('xT', [D, NT])
    memT = din('memT', [D, NSEQ * MEM])
    G.lng_d = din('lng', [128, 8, DC])
    G.lnb_d = din('lnb', [128, 8, DC])
    G.wgu, G.wd = {}, {}
    for l in range(NLAYERS):
        for i in range(2):
            G.wgu[(l, i)] = WT(nc, 'wgu%d%d' % (l, i), FC, DC * 256, 4)
            G.wd[(l, i)] = WT(nc, 'wd%d%d' % (l, i), DC, FC * 128, 2)
    for name, n, F, grp in WSPECS:
        setattr(G, name, [WT(nc, '%s%d' % (name, l), n, F, grp) for l in range(NLAYERS)])
    qnorm_d = din('qnorm', [128, DEPTH, 4])
    kvnorm_d = din('kvnorm', [128, DEPTH, 4])
    G.sgu_g_d = din('sgu_g', [DEPTH, 512])
    G.sgu_b_d = din('sgu_b', [DEPTH, 512])
    G.wsT_d = din('wsT', [DEPTH, 128, 4, 128])
    G.bs_d = din('bs', [DEPTH, 4, 128])
    G.rel_d = din('rel33', [33, 12])
    G.oh_d = din('oh', [33, 3, 385])
    tril_d = din('tril', [128, 128])
    ident_d = din('ident', [128, 128])
    G.cos_d = din('cosT', [64, S])
    G.sin_d = din('sinT', [64, S])

    XT = dint('XT', [D, NT], F32)
    XB = dint('XB', [D, NT], BF16)
    G.OA = dint('OA', [1024, NT], BF16)
    OB = dint('OB', [512, NT], BF16)
    G.OC = dint('OC', [512, NT], BF16)
    MEMB = dint('MEMB', [D, NSEQ * MEM], BF16)
    G.gm_d = dint('gm', [12, 128 * 385], F32)
    G.E_d = dint('E_d', [128, 12, 256], F32)
    yT = nc.dram_tensor('yT', [D, NT], F32, kind='ExternalOutput').ap()

    fm = lambda a: a.rearrange('(c p) t -> p c t', p=128)
    G.XTv, G.XBv, G.OAv, G.OBv, G.OCv, G.MEMBv = fm(XT), fm(XB), fm(G.OA), fm(OB), fm(G.OC), fm(MEMB)
    xTv, yTv = fm(xT), fm(yT)

    with ExitStack() as st:
        P = Prog(nc, st)
        P.G = G
        build_globals(P, nc, G, st)
        G.qnorm = sb(st, nc, 'qnorm_sb', [128, DEPTH, 4], F32)
        G.kvnorm = sb(st, nc, 'kvnorm_sb', [128, DEPTH, 4], F32)
        G.tri_f = sb(st, nc, 'tri_f', [128, 128], F32)
        G.tri_b = sb(st, nc, 'tri_b', [128, 128], BF16)
        G.ident_b = sb(st, nc, 'ident_b', [128, 128], BF16)
        P.dma('sp', out=G.qnorm[:, :, :], in_=qnorm_d, writes=['smallp'])
        P.dma('sp', out=G.kvnorm[:, :, :], in_=kvnorm_d, writes=['smallp'], acc=True)
        P.dma('sp', out=G.tri_f[:, :], in_=tril_d, writes=['tri_f'])
        P.dma('pool', out=G.tri_b[:, :], in_=tril_d, writes=['tri_b'])
        P.dma('pool', out=G.ident_b[:, :], in_=ident_d, writes=['ident_b'])
        nblk = NT // 512

        def xb_job(b):
            return lambda: P.dma('poolc', out=XB[:, b * 512:(b + 1) * 512], in_=xT[:, b * 512:(b + 1) * 512],
                                 writes=['XB_%d' % b])

        castq = [xb_job(0)]
        castq += G.wgu[(0, 0)].jobs()
        castq += G.wd[(0, 0)].jobs()
        for b in range(1, nblk):
            castq.append(xb_job(b))
        castq.append(lambda: P.dma('poolc', out=MEMB[:, :], in_=memT[:, :], writes=['MEMB']))
        n_start = len(castq)
        for l in range(NLAYERS):
            if l > 0:
                castq += G.wgu[(l, 0)].jobs() + G.wd[(l, 0)].jobs()
            for name, _, _, _ in WSPECS:
                castq += getattr(G, name)[l].jobs()
            castq += G.wgu[(l, 1)].jobs() + G.wd[(l, 1)].jobs()

        def pump(n):
            for _ in range(n):
                if castq:
                    j = castq.pop(0)
                    if callable(j):
                        j()
                    else:
                        j[0].issue(P, j[1])

        def need(*ws):
            for w in ws:
                while not w.done:
                    assert castq
                    pump(1)

        G.pump = pump
        pump(n_start)
        G.extra_jobs = e_table_jobs(P, nc, G, st)
        pump(4)

        stop = False
        for l in range(NLAYERS):
            last = (l == NLAYERS - 1)
            phases = ['ffn0', 'mix', 'merge', 'xattn', 'ffn1']
            for pi, ph in enumerate(phases):
                if ph == 'ffn0':
                    need(G.wgu[(l, 0)], G.wd[(l, 0)])
                    ffn_phase(P, nc, G, l, 0, xTv if l == 0 else G.XTv, G.XTv, G.XBv, NT, l * 4 + 0)
                    P.barrier()
                elif ph == 'mix':
                    need(G.win_mla[l], G.win_u[l], G.win_v[l], G.win_qkv[l], G.wuq[l], G.wukv[l])
                    while getattr(G, 'extra_jobs', None):
                        G.extra_jobs.pop(0)()
                    mla_phase(P, nc, G, l, list(range(NSEQ)))
                    P.barrier()
                    sgu_phase(P, nc, G, l, list(range(NSEQ)))
                    P.barrier()
                    dil_phase(P, nc, G, l, list(range(NSEQ)))
                    P.barrier()
                elif ph == 'merge':
                    need(G.win_gate[l], G.wbr[l], G.wout[l])
                    merge_phase(P, nc, G, l, NT, l * 4 + 1)
                    P.barrier()
                elif ph == 'xattn':
                    need(G.xa_wq[l], G.xa_wkv[l], G.xa_wo[l])
                    xattn_phase(P, nc, G, l, list(range(NSEQ)), l * 4 + 2)
                    P.barrier()
                elif ph == 'ffn1':
                    need(G.wgu[(l, 1)], G.wd[(l, 1)])
                    ffn_phase(P, nc, G, l, 1, G.XTv, yTv if (last and upto is None) else G.XTv,
                              None if (last and upto is None) else G.XBv, NT, l * 4 + 3)
                    P.barrier()
                if upto is not None and (l, pi) == tuple(upto):
                    stop = True
                    break
            if stop:
                break
        if upto is not None and dbg:
            for nm, src in (('dOA', G.OA), ('dOB', OB), ('dOC', G.OC)):
                dd = nc.dram_tensor(nm, list(src.shape), BF16, kind='ExternalOutput').ap()
                nb_ = NT // 512
                P.dma('sp', out=dd[:, :], in_=src[:, :], reads=['%s_%d' % (nm[1:], b_) for b_ in range(nb_)], writes=[nm])
        if upto is not None:
            with nc.sbuf_tensor('dump', [128, DC, 512], F32) as dump:
                for b in range(NT // 512):
                    P.dma('sp', out=dump[:, :, :], in_=G.XTv[:, :, b * 512:(b + 1) * 512], reads=['XT_%d' % b], writes=['dump'])
                    P.dma('sp', out=yTv[:, :, b * 512:(b + 1) * 512], in_=dump[:, :, :], reads=['dump'], writes=['yT_%d' % b])
                P.finish()
        else:
            P.finish()
        G.ninstr = P.nins
    return nc, G


def _t5_bucket_np(dist):
    import math
    exact = 16
    df = np.maximum(dist, 1).astype(np.float32)
    large = exact + (np.log(df / np.float32(exact)) / np.float32(math.log(2048 / exact)) * np.float32(32 - exact)).astype(np.int32)
    large = np.minimum(large, 31)
    return np.where(dist < exact, dist, large)


def host_constants():
    c = {}
    oh = np.zeros((33, 3, 385), np.float32)
    for g, dil in enumerate((1, 4, 16)):
        for m in range(385):
            j = m - 127
            if 0 <= j <= 128:
                oh[int(_t5_bucket_np(np.array(j * dil))), g, m] = 1.0
            else:
                oh[32, g, m] = 1.0
    c['oh'] = oh
    k = np.arange(128)
    c['tril'] = (k[:, None] <= k[None, :]).astype(np.float32)
    c['ident'] = np.eye(128, dtype=np.float32)
    inv = (np.float32(10000.0) ** (-np.arange(0, 64, 2, dtype=np.float32) / np.float32(64))).astype(np.float32)
    ang = (np.arange(S, dtype=np.float32)[:, None] * inv[None, :]).astype(np.float32)
    cs, sn = np.cos(ang).astype(np.float32).T, np.sin(ang).astype(np.float32).T
    c['cosT'] = np.ascontiguousarray(np.concatenate([cs, cs], 0))
    c['sinT'] = np.ascontiguousarray(np.concatenate([-sn, sn], 0))
    return c


def host_weights(inp, layers):
    w = {}
    A = lambda a: np.asarray(a, dtype=np.float32)
    sw = np.concatenate([np.arange(32, 64), np.arange(0, 32)])
    for l in layers:
        for i in range(2):
            w['wgu%d%d' % (l, i)] = tile_wgu(A(inp['ffn_wg'][l, i]), A(inp['ffn_wu'][l, i]))
            w['wd%d%d' % (l, i)] = tile_w(A(inp['ffn_wd'][l, i]))
        win = A(inp['w_in'][l])
        kpe = win[:, 1024:1088]
        w['win_mla%d' % l] = tile_w(np.concatenate([win[:, 0:1024], kpe, kpe[:, sw]], 1))
        w['win_u%d' % l] = tile_w(win[:, 1088:1600])
        w['win_v%d' % l] = np.ascontiguousarray(win[:, 1600:2112].reshape(16, 128, 512).transpose(1, 0, 2)).reshape(1, 128, 8192)
        cols = []
        for hh in range(4):
            for t in range(3):
                for g in range(3):
                    h = 4 * g + hh
                    cols.append(np.arange(2112 + t * 1536 + h * 128, 2112 + t * 1536 + (h + 1) * 128))
        w['win_qkv%d' % l] = tile_w(win[:, np.concatenate(cols)])
        w['win_gate%d' % l] = tile_w(win[:, 6720:])
        uq = A(inp['mla_w_uq'][l])
        cols = []
        for h in range(8):
            base = h * 192
            cols += [np.arange(base, base + 128), np.arange(base + 128, base + 192), base + 128 + sw]
        w['wuq%d' % l] = tile_w(uq[:, np.concatenate(cols)])
        w['wukv%d' % l] = tile_w(A(inp['mla_w_ukv'][l]))
        w['wbr%d' % l] = tile_w(A(inp['w_branch'][l]))
        w['wout%d' % l] = tile_w(A(inp['w_out'][l]))
        w['xa_wq%d' % l] = tile_w(A(inp['xa_wq'][l]))
        w['xa_wkv%d' % l] = tile_w(A(inp['xa_wkv'][l]))
        w['xa_wo%d' % l] = tile_w(A(inp['xa_wo'][l]))
    L = DEPTH
    w['lng'] = np.ascontiguousarray(A(inp['ln_g']).reshape(L * 4, DC, 128).transpose(2, 0, 1))
    w['lnb'] = np.ascontiguousarray(A(inp['ln_b']).reshape(L * 4, DC, 128).transpose(2, 0, 1))
    w['qnorm'] = np.ascontiguousarray(A(inp['mla_q_norm']).reshape(L, 4, 128).transpose(2, 0, 1))
    w['kvnorm'] = np.ascontiguousarray(A(inp['mla_kv_norm']).reshape(L, 4, 128).transpose(2, 0, 1))
    w['sgu_g'] = np.ascontiguousarray(A(inp['sgu_ln_g']))
    w['sgu_b'] = np.ascontiguousarray(A(inp['sgu_ln_b']))
    w['wsT'] = np.ascontiguousarray(A(inp['sgu_ws']).transpose(0, 3, 1, 2))
    w['bs'] = np.ascontiguousarray(A(inp['sgu_bs']))
    w['rel33'] = np.ascontiguousarray(np.concatenate([A(inp['rel_bias']), np.full((1, 12), -30000.0, np.float32)], 0))
    w.update(host_constants())
    return w


_CACHE = {}


def kernel(**inputs):
    NCORES = 8
    NSEQ = 16 // NCORES
    if 'nc' not in _CACHE:
        _CACHE['nc'] = build_program(NSEQ=NSEQ, NLAYERS=DEPTH)[0]
    nc = _CACHE['nc']
    shared = host_weights(inputs, range(DEPTH))
    x = np.asarray(inputs['x'], dtype=np.float32)
    mem = np.asarray(inputs['mem'], dtype=np.float32)
    in_maps = []
    for c in range(NCORES):
        m = dict(shared)
        m['xT'] = np.ascontiguousarray(x[c * NSEQ:(c + 1) * NSEQ].reshape(NSEQ * S, D).T)
        m['memT'] = np.ascontiguousarray(mem[c * NSEQ:(c + 1) * NSEQ].reshape(NSEQ * MEM, D).T)
        in_maps.append(m)
    res = run_bass_kernel_spmd(nc, in_maps, core_ids=list(range(NCORES)))
    out = np.empty((16, S, D), np.float32)
    for c in range(NCORES):
        out[c * NSEQ:(c + 1) * NSEQ] = res.results[c]['yT'].T.reshape(NSEQ, S, D)
    return out
```
